# Optimizing a Trainium2 kernel written in Bass

```python
import math
import jax, jax.numpy as jnp
from jax import lax
import numpy as np

D_MODEL = 1024
BATCH = 4
SEQ = 4096
DEPTH = 1
DEC_BATCH = 128
DEC_SEQ = 4
PAST_LEN = 2048
PAGE_SIZE = 128

D_MIX = D_MODEL
D_LIN = D_MIX // 2
D_ATT = D_MIX - D_LIN
LIN_HEAD_DIM = 128
LIN_HEADS = D_LIN // LIN_HEAD_DIM
ATT_HEAD_DIM = 64
ATT_HEADS = D_ATT // ATT_HEAD_DIM
MOBA_BLOCK = 256
MOBA_TOPK = 3
MOBA_Q_ROWS = 128
HGRN_CHUNK = 64
N_BUCKETS = 32
MAX_DISTANCE = 1024
EPS = 1e-6
D_IN = 4 * D_LIN + 4 * D_ATT
SPLITS = [D_LIN, 2 * D_LIN, 3 * D_LIN, 4 * D_LIN,
          4 * D_LIN + D_ATT, 4 * D_LIN + 2 * D_ATT, 4 * D_LIN + 3 * D_ATT]

kernel_name = "hymba_hgrn2_moba_decode_step"


def rmsnorm(x, g):
    xf = x.astype(jnp.float32)
    xf = xf * lax.rsqrt(jnp.mean(xf * xf, axis=-1, keepdims=True) + EPS)
    return (xf * g.astype(jnp.float32)).astype(x.dtype)


def hgrn2_scan(q, log_f, k, v, s0):
    B, T, H, DK = q.shape
    DV = v.shape[-1]
    C = math.gcd(T, HGRN_CHUNK)
    N = T // C

    def to_chunks(a):
        return a.reshape(B, N, C, H, a.shape[-1]).transpose(1, 0, 3, 2, 4)

    causal = jnp.tril(jnp.ones((C, C), dtype=bool))[:, :, None]

    def step(S, inp):
        qi, gi, ki, vi = inp
        b = jnp.cumsum(gi, axis=2)
        inter = jnp.einsum('bhtk,bhkv->bhtv', qi * jnp.exp(b), S)
        rel = b[:, :, :, None, :] - b[:, :, None, :, :]
        decay = jnp.exp(jnp.where(causal, rel, -jnp.inf))
        att = jnp.einsum('bhtk,bhsk,bhtsk->bhts', qi, ki, decay)
        intra = jnp.einsum('bhts,bhsv->bhtv', att, vi)
        b_last = b[:, :, -1:, :]
        S_new = jnp.exp(b_last[:, :, 0, :])[..., None] * S + jnp.einsum(
            'bhsk,bhsv->bhkv', ki * jnp.exp(b_last - b), vi)
        return S_new, inter + intra

    S_final, o = lax.scan(step, s0, (to_chunks(q), to_chunks(log_f), to_chunks(k), to_chunks(v)))
    o = o.transpose(1, 0, 3, 2, 4).reshape(B, T, H, DV)
    return o, S_final


def t5_bucket(dist):
    max_exact = N_BUCKETS // 2
    n = jnp.maximum(dist, 0)
    large = max_exact + (jnp.log(jnp.maximum(n, 1).astype(jnp.float32) / max_exact)
                         / math.log(MAX_DISTANCE / max_exact) * (N_BUCKETS - max_exact)).astype(jnp.int32)
    large = jnp.minimum(large, N_BUCKETS - 1)
    return jnp.where(n < max_exact, n, large)


def moba_attention(q, k, v, q_pos, rel_bias):
    B, T, H, hd = q.shape
    L = k.shape[1]
    nb = -(-L // MOBA_BLOCK)
    pad = nb * MOBA_BLOCK - L
    k = jnp.pad(k, ((0, 0), (0, pad), (0, 0), (0, 0)))
    v = jnp.pad(v, ((0, 0), (0, pad), (0, 0), (0, 0)))
    kbt = k.reshape(B, nb, MOBA_BLOCK, H, hd).transpose(0, 3, 1, 2, 4)
    vbt = v.reshape(B, nb, MOBA_BLOCK, H, hd).transpose(0, 3, 1, 2, 4)
    k_mean = jnp.mean(kbt.astype(jnp.float32), axis=3)
    n_sel = min(MOBA_TOPK, nb)
    qb = math.gcd(T, max(1, MOBA_Q_ROWS // B))
    n_qb = T // qb
    q_blocks = q.reshape(B, n_qb, qb, H, hd).transpose(1, 0, 3, 2, 4)
    pos_blocks = q_pos.reshape(n_qb, qb)
    bi = jnp.arange(B)[:, None, None, None]
    hi = jnp.arange(H)[None, :, None, None]
    blk_ids = jnp.arange(nb)
    offs = jnp.arange(MOBA_BLOCK)
    scale = hd ** -0.5

    def attend_block(args):
        qi, pos = args
        own = pos // MOBA_BLOCK
        gate = jnp.einsum('bhqd,bhnd->bhqn', qi.astype(jnp.float32), k_mean)
        fully_past = blk_ids[None, :] < own[:, None]
        gate = jnp.where(fully_past, gate, -jnp.inf)
        top_s, top_i = lax.top_k(gate, n_sel)
        sel = jnp.concatenate([top_i, jnp.broadcast_to(own[:, None], (B, H, qb, 1))], axis=-1)
        valid = jnp.concatenate([jnp.isfinite(top_s), jnp.ones((B, H, qb, 1), dtype=bool)], axis=-1)
        kg = kbt[bi, hi, sel]
        vg = vbt[bi, hi, sel]
        dist = pos[:, None, None] - (sel[..., None] * MOBA_BLOCK + offs)
        logits = (jnp.einsum('bhqd,bhqjpd->bhqjp', qi, kg).astype(jnp.float32) * scale
                  + rel_bias[t5_bucket(dist), hi[..., None]].astype(jnp.float32))
        logits = jnp.where(valid[..., None] & (dist >= 0), logits, -jnp.inf)
        p = jax.nn.softmax(logits.reshape(B, H, qb, -1), axis=-1).reshape(logits.shape)
        return jnp.einsum('bhqjp,bhqjpd->bqhd', p.astype(vg.dtype), vg)

    out = lax.map(attend_block, (q_blocks, pos_blocks))
    return out.transpose(1, 0, 2, 3, 4).reshape(B, T, H, hd)


def hybrid_layer(x, k_past, v_past, s0, q_pos, w_in, w_out, g_pre, g_post, g_lin, lb, rel_bias):
    B, T, _ = x.shape
    h = rmsnorm(x, g_pre)
    z = jnp.einsum('btd,de->bte', h, w_in)
    lq, lf, li, lg, aq, ak, av, ag = jnp.split(z, SPLITS, axis=-1)
    f = lb + (1.0 - lb) * jax.nn.sigmoid(lf.astype(jnp.float32))
    shp = (B, T, LIN_HEADS, LIN_HEAD_DIM)
    o_lin, s_new = hgrn2_scan(jax.nn.silu(lq.astype(jnp.float32)).reshape(shp),
                              jnp.log(f).reshape(shp), (1.0 - f).reshape(shp),
                              li.astype(jnp.float32).reshape(shp), s0.astype(jnp.float32))
    o_lin = rmsnorm(o_lin, g_lin).reshape(B, T, D_LIN).astype(x.dtype) * jax.nn.silu(lg)
    ashp = (B, T, ATT_HEADS, ATT_HEAD_DIM)
    qa, ka, va = aq.reshape(ashp), ak.reshape(ashp), av.reshape(ashp)
    k_all = ka if k_past is None else jnp.concatenate([k_past.astype(ka.dtype), ka], axis=1)
    v_all = va if v_past is None else jnp.concatenate([v_past.astype(va.dtype), va], axis=1)
    o_att = moba_attention(qa, k_all, v_all, q_pos, rel_bias).reshape(B, T, D_ATT) * jax.nn.silu(ag)
    o = jnp.einsum('bte,ed->btd', jnp.concatenate([o_lin, o_att], axis=-1), w_out)
    y = x + rmsnorm(o, g_post)
    return y, ka, va, s_new.astype(s0.dtype)


def setup_inputs(seed: int = 0) -> dict:
    key = jax.random.key(seed)
    ks = jax.random.split(key, 14)
    n_pages = PAST_LEN // PAGE_SIZE
    n_phys = (DEC_BATCH * n_pages * 5) // 4
    f32 = jnp.float32
    perm = jax.random.permutation(ks[0], n_phys)[:DEC_BATCH * n_pages]
    return {
        "x_prompt": jax.random.normal(ks[1], (BATCH, SEQ, D_MODEL), f32),
        "x_sample": jax.random.normal(ks[2], (DEC_BATCH, DEC_SEQ, D_MODEL), f32),
        "cache_k": jax.random.normal(ks[3], (DEPTH, n_phys, PAGE_SIZE, ATT_HEADS, ATT_HEAD_DIM), f32),
        "cache_v": jax.random.normal(ks[4], (DEPTH, n_phys, PAGE_SIZE, ATT_HEADS, ATT_HEAD_DIM), f32),
        "state_hgrn": 0.5 * jax.random.normal(ks[5], (DEPTH, DEC_BATCH, LIN_HEADS, LIN_HEAD_DIM, LIN_HEAD_DIM), f32),
        "page_table": perm.reshape(DEC_BATCH, n_pages).astype(jnp.int32),
        "w_in": jax.random.normal(ks[6], (DEPTH, D_MODEL, D_IN), f32) * D_MODEL ** -0.5,
        "w_out": jax.random.normal(ks[7], (DEPTH, D_MIX, D_MODEL), f32) * D_MIX ** -0.5,
        "norm_pre": 1.0 + 0.05 * jax.random.normal(ks[8], (DEPTH, D_MODEL), f32),
        "norm_post": 1.0 + 0.05 * jax.random.normal(ks[9], (DEPTH, D_MODEL), f32),
        "norm_lin_out": 1.0 + 0.05 * jax.random.normal(ks[10], (DEPTH, LIN_HEAD_DIM), f32),
        "lin_lower_bound": jax.random.normal(ks[11], (DEPTH + 1, D_LIN), f32),
        "rel_bias": 0.5 * jax.random.normal(ks[12], (N_BUCKETS, ATT_HEADS), f32),
    }


def reference(x_prompt, x_sample, cache_k, cache_v, state_hgrn, page_table, w_in, w_out,
              norm_pre, norm_post, norm_lin_out, lin_lower_bound, rel_bias):
    n_pages = page_table.shape[1]
    past_len = n_pages * PAGE_SIZE
    pos_prompt = jnp.arange(x_prompt.shape[1], dtype=jnp.int32)
    pos_sample = past_len + jnp.arange(x_sample.shape[1], dtype=jnp.int32)
    lb_all = jnp.cumsum(jax.nn.softmax(lin_lower_bound.astype(jnp.float32), axis=0), axis=0)
    yp, ys = x_prompt, x_sample
    kp_l, vp_l, sp_l, ks_l, vs_l, ss_l = [], [], [], [], [], []
    for l in range(DEPTH):
        lb = lb_all[l]
        Bp = yp.shape[0]
        s0_p = jnp.zeros((Bp, LIN_HEADS, LIN_HEAD_DIM, LIN_HEAD_DIM), state_hgrn.dtype)
        yp, kp, vp, sp = hybrid_layer(yp, None, None, s0_p, pos_prompt, w_in[l], w_out[l],
                                      norm_pre[l], norm_post[l], norm_lin_out[l], lb, rel_bias)
        Bs = ys.shape[0]
        k_past = cache_k[l][page_table].reshape(Bs, past_len, ATT_HEADS, ATT_HEAD_DIM)
        v_past = cache_v[l][page_table].reshape(Bs, past_len, ATT_HEADS, ATT_HEAD_DIM)
        ys, ksn, vsn, ssn = hybrid_layer(ys, k_past, v_past, state_hgrn[l], pos_sample, w_in[l], w_out[l],
                                         norm_pre[l], norm_post[l], norm_lin_out[l], lb, rel_bias)
        kp_l.append(kp); vp_l.append(vp); sp_l.append(sp)
        ks_l.append(ksn); vs_l.append(vsn); ss_l.append(ssn)
    k_prompt_new = jnp.stack(kp_l)
    v_prompt_new = jnp.stack(vp_l)
    state_prompt_new = jnp.stack(sp_l)
    k_sample_new = jnp.stack(ks_l)
    v_sample_new = jnp.stack(vs_l)
    state_sample_new = jnp.stack(ss_l)
    return (yp, ys, k_prompt_new, v_prompt_new, state_prompt_new, k_sample_new, v_sample_new, state_sample_new)
```

```python
import contextlib
import math
import numpy as np
import ml_dtypes
import concourse.bass as bass
import concourse.mybir as mybir
from concourse.bass_utils import run_bass_kernel_spmd

F32 = mybir.dt.float32
BF16 = mybir.dt.bfloat16
I32 = mybir.dt.int32
AF = mybir.ActivationFunctionType
ALU = mybir.AluOpType
AX = mybir.AxisListType

NT = 33
NTOK = NT * 128
EPS = 1e-6
NEG = -32768.0
WB = 1792
JOFF = 511
TABW = WB + 128
NPHYS = 2560
DEBUG_STOP = 0
ENABLE_QK = True
ENABLE_C = True
ENABLE_D = True
GROUPS = [[0, 1], [2, 3], [4, 5], [6, 7]]


class Buf:
    def __init__(self, name, t):
        self.name = name
        self.t = t
        self.w = []
        self.r = []
        self.dsem = None
        self.root = self
        self.excl = False

    def view(self, ap):
        v = Buf(self.name + "_v", ap)
        v.root = self.root
        return v


class Prog:
    def __init__(self, nc):
        self.nc = nc
        self.es = contextlib.ExitStack()
        self.eng = {"pe": nc.tensor, "act": nc.scalar, "dve": nc.vector, "pool": nc.gpsimd, "sp": nc.sync}
        self.sem = {k: self.es.enter_context(nc.semaphore("s_" + k)) for k in self.eng}
        self.cnt = {k: 0 for k in self.eng}
        self.known = {k: {} for k in self.eng}
        self.dsems = []
        self.dcnt = {}
        self.nbuf = 0

    def sb(self, name, shape, dt, es=None):
        t = (es or self.es).enter_context(self.nc.sbuf_tensor(name, shape, dt))
        return Buf(name, t)

    def ps(self, name, shape, dt, es=None):
        t = (es or self.es).enter_context(self.nc.psum_tensor(name, shape, dt))
        b = Buf(name, t)
        b.excl = True
        return b

    def dram(self, name, shape, dt, kind=None):
        if kind is None:
            t = self.nc.dram_tensor(name, shape, dt)
        else:
            t = self.nc.dram_tensor(name, shape, dt, kind=kind)
        return Buf(name, t)

    def _dsem(self, buf):
        buf = buf.root
        if buf.dsem is None:
            s = self.es.enter_context(self.nc.semaphore("d%d" % len(self.dsems)))
            self.dsems.append(s)
            self.dcnt[id(s)] = 0
            buf.dsem = s
        return buf.dsem

    def _waits(self, e, reads, writes):
        need = {}

        def add(ev):
            s, v = ev
            if need.get(id(s), (s, 0))[1] < v:
                need[id(s)] = (s, v)
        for b in reads:
            for ev in b.root.w:
                add(ev)
        for b in writes:
            for ev in b.root.w:
                add(ev)
            for ev in b.root.r:
                add(ev)
        kn = self.known[e]
        eng = self.eng[e]
        for k, (s, v) in need.items():
            if e == "pe" and s is self.sem["pe"]:
                continue
            if kn.get(k, 0) >= v:
                continue
            eng.wait_ge(s, v)
            kn[k] = v

    def _record(self, ev, reads, writes):
        for b in reads:
            b = b.root
            b.r = [x for x in b.r if x[0] is not ev[0]] + [ev]
        for b in writes:
            b = b.root
            b.w = [ev]
            b.r = []

    def op(self, e, name, reads, writes, **kw):
        writes = list(writes) + [b for b in reads if b.root.excl]
        self._waits(e, reads, writes)
        ins = getattr(self.eng[e], name)(**kw)
        self.cnt[e] += 1
        ins.then_inc(self.sem[e], 1)
        self._record((self.sem[e], self.cnt[e]), reads, writes)

    def dma(self, e, reads, writes, sem_of, n=1, fn=None, **kw):
        self._waits(e, reads, writes)
        s = self._dsem(sem_of)
        if fn is None:
            self.eng[e].dma_start(**kw).then_inc(s, 16)
        else:
            n = fn(self.eng[e], s)
        self.dcnt[id(s)] += 16 * n
        self._record((s, self.dcnt[id(s)]), reads, writes)

    def barrier(self):
        evs = [(self.sem[k], self.cnt[k]) for k in self.eng] + [(s, self.dcnt[id(s)]) for s in self.dsems]
        for e in self.eng:
            kn = self.known[e]
            for s, v in evs:
                if v == 0 or kn.get(id(s), 0) >= v:
                    continue
                if s is self.sem[e] and e == "pe":
                    continue
                self.eng[e].wait_ge(s, v)
                kn[id(s)] = v

    def finish(self):
        self.barrier()
        self.es.close()


def t5_bucket_np(n):
    n = np.maximum(n, 0).astype(np.int64)
    nf = np.maximum(n, 1).astype(np.float32)
    large = 16 + (np.log(nf / np.float32(16)) / np.float32(math.log(1024 / 16)) * np.float32(16)).astype(np.int32)
    large = np.minimum(large, 31)
    return np.where(n < 16, n, large)


def build_program(phases=("A", "C", "D", "E")):
    nc = bass.Bass("TRN2", target_bir_lowering=False)
    P = Prog(nc)
    din = lambda name, shape, dt=F32: P.dram(name, shape, dt, kind="ExternalInput")
    dout = lambda name, shape, dt=F32: P.dram(name, shape, dt, kind="ExternalOutput")
    xall = din("xall", [NTOK, 1024])
    w_in = din("w_in", [1024, 2048])
    w_out = din("w_out", [1024, 1024])
    gpre = din("gpre", [128, 8])
    gpost = din("gpost", [128, 1024])
    glin = din("glin", [128, 256])
    llb = din("llb", [128, 512])
    relb = din("relb", [32, 4])
    NP = NTOK - 128
    NBLK = NP // 256
    onehot = din("onehot", [33, TABW])
    consts = din("consts", [128, 1024])
    kind_c = din("kind", [16, NTOK], BF16)
    pmask = din("pmask", [128, 3 * 16 * 16])
    cache_k = din("cache_k", [NPHYS * 128, 256])
    cache_v = din("cache_v", [NPHYS * 128, 256])
    ptab = din("ptab", [128, 512], I32)
    ohn = din("ohn", [33, 4 * 128])
    kind2 = din("kind2", [16, 16 * 512], BF16)
    state_in = din("state_in", [64, 128, 128])
    y_out = dout("y", [NTOK, 1024])
    k_out = dout("k_new", [NTOK, 256])
    v_out = dout("v_new", [NTOK, 256])
    stp_out = dout("st_p", [2, 128, 128])
    sts_out = dout("st_s", [64, 128, 128])
    CHT = 11
    NCH = (NT + CHT - 1) // CHT
    chw = [min(CHT, NT - CHT * k) * 128 for k in range(NCH)]
    ibs = [P.dram("ib%d" % k, [512, chw[k]], BF16) for k in range(NCH)]
    obs = [P.dram("ob%d" % k, [1024, chw[k]], BF16) for k in range(NCH)]

    def ib_tile(r0, i):
        k, c = i // CHT, (i % CHT) * 128
        return ibs[k], ibs[k].t[r0:r0 + 256, c:c + 128].rearrange("(h p) t -> p h t", h=2)
    btab = P.dram("btab", [4, 128, TABW], F32)

    ident_b = P.sb("ident_b", [128, 128], BF16)
    ident_f = P.sb("ident_f", [128, 128], F32)
    cst = P.sb("cst", [128, 1024], F32)
    tri_p = cst.t[:, 0:128]
    tri_s = cst.t[:, 128:256]
    bmask = cst.t[:, 512:544]
    qT = P.sb("qT", [80, 4, NTOK], BF16)
    kT = P.sb("kT", [80, 4, NTOK], BF16)
    vaug = P.sb("vaug", [128, NT, 4, 65], BF16)
    sag = P.sb("sag", [128, NT, 256], BF16)
    lbt = P.sb("lbt", [128, 2, 256], F32)
    glin_s = P.sb("glin_s", [128, 256], F32)
    gpre_s = P.sb("gpre_s", [128, 8], F32)
    epsb = P.sb("epsb", [128, 2], F32)
    Bsall = P.sb("Bsall", [128, 256], F32)
    Bnn = P.sb("Bnn", [128, 16], F32)

    P.dma("sp", [], [cst], cst, out=cst.t[:], in_=consts.t[:, :])
    P.dma("sp", [], [glin_s], glin_s, out=glin_s.t[:], in_=glin.t[:, :])
    P.dma("sp", [], [gpre_s], gpre_s, out=gpre_s.t[:], in_=gpre.t[:, :])
    P.dma("sp", [], [kT], kT, out=kT.t[64:80, 0, :], in_=kind_c.t[:, :])
    P.dma("sp", [], [kT], kT, out=kT.t[64:80, 1, :], in_=kind_c.t[:, :])
    P.dma("sp", [], [kT], kT, out=kT.t[64:80, 2, :], in_=kind_c.t[:, :])
    P.dma("sp", [], [kT], kT, out=kT.t[64:80, 3, :], in_=kind_c.t[:, :])
    P.op("pool", "memset", [], [epsb], ap=epsb.t[:, 0:1], constant=EPS)
    P.op("pool", "memset", [], [epsb], ap=epsb.t[:, 1:2], constant=1.0)
    P.op("pool", "memset", [], [vaug], ap=vaug.t[:, :, :, 64:65], constant=1.0)
    if DEBUG_STOP == 10:
        P.finish()
        return nc
    P.op("dve", "tensor_copy", [cst], [ident_f], out=ident_f.t[:], in_=cst.t[:, 256:384])
    P.op("dve", "tensor_copy", [cst], [ident_b], out=ident_b.t[:], in_=cst.t[:, 256:384])
    es0 = contextlib.ExitStack()
    llb_s = P.sb("llb_s", [128, 512], F32, es0)
    P.dma("sp", [], [llb_s], llb_s, out=llb_s.t[:], in_=llb.t[:, :])
    P.op("dve", "tensor_tensor", [llb_s], [lbt], out=lbt.t[:, 1, :], in0=llb_s.t[:, 256:512], in1=llb_s.t[:, 0:256], op=ALU.subtract)
    P.op("act", "activation", [lbt], [lbt], out=lbt.t[:, 1, :], in_=lbt.t[:, 1, :], func=AF.Exp)
    P.op("dve", "tensor_scalar", [lbt], [lbt], out=lbt.t[:, 1, :], in0=lbt.t[:, 1, :], scalar1=1.0, scalar2=None, op0=ALU.add)
    P.op("dve", "reciprocal", [lbt], [lbt], out=lbt.t[:, 0, :], in_=lbt.t[:, 1, :])
    P.op("dve", "tensor_scalar", [lbt], [lbt], out=lbt.t[:, 1, :], in0=lbt.t[:, 0, :], scalar1=-1.0, scalar2=1.0, op0=ALU.mult, op1=ALU.add)

    if DEBUG_STOP == 11:
        P.barrier(); es0.close(); P.finish()
        return nc
    esA = contextlib.ExitStack()
    Wb = P.sb("Wb", [128, 8, 2048], BF16, esA)
    esW = contextlib.ExitStack()
    wst = [P.sb("wst%d" % i, [128, 2048], F32, esW) for i in range(2)]
    for j in range(8):
        w = wst[j % 2]
        P.dma("sp", [], [w], w, out=w.t[:], in_=w_in.t[j * 128:(j + 1) * 128, :])
        if j % 2 == 0:
            P.op("dve", "tensor_scalar", [w, gpre_s], [Wb], out=Wb.t[:, j, :], in0=w.t[:],
                 scalar1=gpre_s.t[:, j:j + 1], scalar2=None, op0=ALU.mult)
        else:
            P.op("act", "activation", [w, gpre_s], [Wb], out=Wb.t[:, j, :], in_=w.t[:], func=AF.Copy, scale=gpre_s.t[:, j:j + 1])
    P.barrier()
    esW.close()
    if DEBUG_STOP == 1:
        P.barrier(); esA.close(); es0.close(); P.finish()
        return nc

    xt = [P.sb("xt%d" % i, [128, 1024], F32, esA) for i in range(2)]
    junk = P.sb("junk", [128, 1024], F32, esA)
    hb = P.sb("hb", [128, 1024], BF16, esA)
    hT = P.sb("hT", [128, 8, 128], BF16, esA)
    ss = P.sb("ss", [128, 8], F32, esA)
    E0 = P.sb("E0", [128, 512], F32, esA)
    E1 = P.sb("E1", [128, 512], F32, esA)
    qf = P.sb("qf", [128, 256], F32, esA)
    ff = P.sb("ff", [128, 256], F32, esA)
    gg = P.sb("gg", [128, 256], F32, esA)
    kk = P.sb("kk", [128, 256], F32, esA)
    sgl = P.sb("sgl", [128, 256], F32, esA)
    vlin = P.sb("vlin", [128, 256], BF16, esA)
    vo = [P.sb("vo%d" % i, [128, 256], F32, esA) for i in range(2)]
    ko = [P.sb("ko%d" % i, [128, 256], F32, esA) for i in range(2)]
    qkb = P.sb("qkb", [128, 512], BF16, esA)
    EB = P.sb("EB", [128, 2, 256], F32, esA)
    qkt = P.sb("qkt", [128, 2, 256], BF16, esA)
    qkT = P.sb("qkT", [128, 4, 128], BF16, esA)
    attm = P.sb("attm", [128, 2, 128], BF16, esA)
    qTa = P.sb("qTa", [128, 2, 128], BF16, esA)
    qTb = P.sb("qTb", [128, 2, 128], BF16, esA)
    ktm = P.sb("ktm", [128, 2, 256], BF16, esA)
    ebl = P.sb("ebl", [128, 2, 2], F32, esA)
    Sf = [P.sb("Sf%d" % h, [128, 128], F32, esA) for h in range(2)]
    Sb = [P.sb("Sb%d" % h, [128, 128], BF16, esA) for h in range(2)]
    tmpS = [P.sb("tmpS%d" % h, [128, 128], F32, esA) for h in range(2)]
    osq = P.sb("osq", [128, 2], F32, esA)
    olg = P.sb("olg", [128, 256], BF16, esA)
    glsg = P.sb("glsg", [128, 256], F32, esA)
    omT = [P.sb("omT%d" % i, [128, 2, 128], BF16, esA) for i in range(2)]
    psT = P.ps("psT", [128, 8, 128], BF16, esA)
    psZ = [P.ps("psZ%d" % i, [128, 512], F32, esA) for i in range(4)]
    bankF = P.ps("bankF", [128, 512], F32, esA)
    bankG = P.ps("bankG", [128, 512], F32, esA)
    bankH = P.ps("bankH", [128, 512], F32, esA)
    psB = bankF.view(bankF.t[:, 0:256])
    psAtt = bankF.view(bankF.t[:, 256:512].rearrange("p (h t) -> p h t", h=2))
    psBT = bankG.view(bankG.t[:, 0:256].rearrange("p (h t) -> p h t", h=2))
    psO = bankG.view(bankG.t[:, 256:512].rearrange("p (h t) -> p h t", h=2))
    psKV = bankH.view(bankH.t[:, 0:256].rearrange("p (h t) -> p h t", h=2))
    psTr = psT

    P.op("pool", "memset", [], [qTa], ap=qTa.t[:], constant=0.0)
    P.op("pool", "memset", [], [qTb], ap=qTb.t[:], constant=0.0)
    for h in range(2):
        P.op("pool", "memset", [], [Sf[h]], ap=Sf[h].t[:], constant=0.0)
        P.op("pool", "memset", [], [Sb[h]], ap=Sb[h].t[:], constant=0.0)

    def load_x(i):
        P.dma("sp", [], [xt[i % 2]], xt[i % 2], out=xt[i % 2].t[:], in_=xall.t[i * 128:(i + 1) * 128, :])

    load_x(0)
    for i in range(NT):
        sample = (i == NT - 1)
        x = xt[i % 2]
        if i + 1 < NT:
            load_x(i + 1)
        P.op("act", "activation", [x], [junk, ss], out=junk.t[:], in_=x.t[:], func=AF.Square, accum_out=ss.t[:, 0:1])
        P.op("act", "activation", [ss, epsb], [ss], out=ss.t[:, 1:2], in_=ss.t[:, 0:1], func=AF.Ln, scale=1.0 / 1024, bias=epsb.t[:, 0:1])
        P.op("act", "activation", [ss], [ss], out=ss.t[:, 2:3], in_=ss.t[:, 1:2], func=AF.Exp, scale=-0.5)
        P.op("dve", "tensor_scalar", [x, ss], [hb], out=hb.t[:], in0=x.t[:], scalar1=ss.t[:, 2:3], scalar2=None, op0=ALU.mult)
        for j in range(8):
            P.op("pe", "transpose", [hb, ident_b], [psT], out=psT.t[:, j, :], in_=hb.t[:, j * 128:(j + 1) * 128], identity=ident_b.t[:])
        P.op("act", "copy", [psT], [hT], out=hT.t[:], in_=psT.t[:])
        for cc in range(4):
            for j in range(8):
                P.op("pe", "matmul", [hT, Wb], [psZ[cc]], out=psZ[cc].t[:], lhsT=hT.t[:, j, :], rhs=Wb.t[:, j, cc * 512:(cc + 1) * 512],
                     start=(j == 0), stop=(j == 7))
        if DEBUG_STOP == 2:
            P.barrier(); esA.close(); es0.close(); P.finish()
            return nc
        P.op("act", "copy", [psZ[3]], [ko[i % 2]], out=ko[i % 2].t[:], in_=psZ[3].t[:, 256:512])
        P.dma("sp", [ko[i % 2]], [], ko[i % 2], out=k_out.t[i * 128:(i + 1) * 128, :], in_=ko[i % 2].t[:])
        if DEBUG_STOP == 20:
            P.barrier(); esA.close(); es0.close(); P.finish()
            return nc
        P.op("dve", "tensor_copy", [psZ[3]], [qkb], out=qkb.t[:], in_=psZ[3].t[:])
        for (dst, off) in (() if not ENABLE_QK else ((qT, 0), (kT, 256))):
            for hq in range(4):
                P.op("pe", "matmul", [qkb, ident_b], [bankH], out=bankH.t[0:64, hq * 128:(hq + 1) * 128], lhsT=qkb.t[:, off + hq * 64:off + (hq + 1) * 64],
                     rhs=ident_b.t[:], start=True, stop=True)
            for hq in range(4):
                P.op("dve", "tensor_copy", [bankH], [dst], out=dst.t[0:64, hq, i * 128:(i + 1) * 128], in_=bankH.t[0:64, hq * 128:(hq + 1) * 128])
        if DEBUG_STOP == 22:
            P.barrier(); esA.close(); es0.close(); P.finish()
            return nc
        P.op("act", "copy", [psZ[2]], [vlin], out=vlin.t[:], in_=psZ[2].t[:, 0:256])
        P.op("act", "copy", [psZ[2]], [vo[i % 2]], out=vo[i % 2].t[:], in_=psZ[2].t[:, 256:512])
        P.dma("sp", [vo[i % 2]], [], vo[i % 2], out=v_out.t[i * 128:(i + 1) * 128, :], in_=vo[i % 2].t[:])
        if DEBUG_STOP == 23:
            P.barrier(); esA.close(); es0.close(); P.finish()
            return nc
        for hq in range(4):
            P.op("dve", "tensor_copy", [vo[i % 2]], [vaug], out=vaug.t[:, i, hq, 0:64], in_=vo[i % 2].t[:, hq * 64:(hq + 1) * 64])
        if DEBUG_STOP == 3:
            P.barrier(); esA.close(); es0.close(); P.finish()
            return nc
        P.op("act", "activation", [psZ[0]], [E0], out=E0.t[:], in_=psZ[0].t[:], func=AF.Exp, scale=-1.0)
        P.op("dve", "tensor_scalar", [E0], [E0], out=E0.t[:], in0=E0.t[:], scalar1=1.0, scalar2=None, op0=ALU.add)
        P.op("dve", "reciprocal", [E0], [E0], out=E0.t[:], in_=E0.t[:])
        P.op("dve", "tensor_tensor", [psZ[0], E0], [qf], out=qf.t[:], in0=psZ[0].t[:, 0:256], in1=E0.t[:, 0:256], op=ALU.mult)
        P.op("dve", "tensor_tensor", [E0, lbt], [ff], out=ff.t[:], in0=E0.t[:, 256:512], in1=lbt.t[:, 1, :], op=ALU.mult)
        P.op("dve", "tensor_tensor", [ff, lbt], [ff], out=ff.t[:], in0=ff.t[:], in1=lbt.t[:, 0, :], op=ALU.add)
        P.op("act", "activation", [ff], [gg], out=gg.t[:], in_=ff.t[:], func=AF.Ln)
        P.op("dve", "tensor_scalar", [ff], [kk], out=kk.t[:], in0=ff.t[:], scalar1=-1.0, scalar2=1.0, op0=ALU.mult, op1=ALU.add)
        P.op("act", "activation", [psZ[1]], [E1], out=E1.t[:], in_=psZ[1].t[:], func=AF.Exp, scale=-1.0)
        P.op("dve", "tensor_scalar", [E1], [E1], out=E1.t[:], in0=E1.t[:], scalar1=1.0, scalar2=None, op0=ALU.add)
        P.op("dve", "reciprocal", [E1], [E1], out=E1.t[:], in_=E1.t[:])
        P.op("dve", "tensor_tensor", [psZ[1], E1], [sgl], out=sgl.t[:], in0=psZ[1].t[:, 0:256], in1=E1.t[:, 0:256], op=ALU.mult)
        P.op("dve", "tensor_tensor", [psZ[1], E1], [sag], out=sag.t[:, i, :], in0=psZ[1].t[:, 256:512], in1=E1.t[:, 256:512], op=ALU.mult)
        P.op("dve", "tensor_tensor", [sgl, glin_s], [glsg], out=glsg.t[:], in0=sgl.t[:], in1=glin_s.t[:], op=ALU.mult)
        if DEBUG_STOP == 4:
            P.barrier(); esA.close(); es0.close(); P.finish()
            return nc
        tri = tri_s if sample else tri_p
        P.op("pe", "matmul", [cst, gg], [psB], out=psB.t[:], lhsT=tri, rhs=gg.t[:], start=True, stop=True)
        for h in range(2):
            P.op("pe", "matmul", [cst, gg], [psBT], out=psBT.t[:, h, :], lhsT=gg.t[:, h * 128:(h + 1) * 128], rhs=tri, start=True, stop=True)
        P.op("act", "activation", [psB], [EB], out=EB.t[:, 0, :], in_=psB.t[:], func=AF.Exp)
        P.op("act", "activation", [psB], [EB], out=EB.t[:, 1, :], in_=psB.t[:], func=AF.Exp, scale=-1.0)
        P.op("dve", "tensor_tensor", [qf, EB], [qkt], out=qkt.t[:, 0, :], in0=qf.t[:], in1=EB.t[:, 0, :], op=ALU.mult)
        P.op("dve", "tensor_tensor", [kk, EB], [qkt], out=qkt.t[:, 1, :], in0=kk.t[:], in1=EB.t[:, 1, :], op=ALU.mult)
        if not sample:
            for h in range(2):
                for c2 in range(2):
                    P.op("act", "activation", [psBT], [ebl], out=ebl.t[:, h, c2:c2 + 1], in_=psBT.t[:, h, 64 * c2 + 63:64 * c2 + 64], func=AF.Exp)
        for h in range(2):
            P.op("pe", "transpose", [qkt, ident_b], [psTr], out=psTr.t[:, 2 * h, :], in_=qkt.t[:, 0, h * 128:(h + 1) * 128], identity=ident_b.t[:])
            P.op("pe", "transpose", [qkt, ident_b], [psTr], out=psTr.t[:, 2 * h + 1, :], in_=qkt.t[:, 1, h * 128:(h + 1) * 128], identity=ident_b.t[:])
        for h4 in range(4):
            P.op("act", "copy", [psTr], [qkT], out=qkT.t[:, h4, :], in_=psTr.t[:, h4, :])
        if not sample:
            for h in range(2):
                P.op("dve", "tensor_copy", [psTr], [qTa], out=qTa.t[:, h, 0:64], in_=psTr.t[:, 2 * h, 0:64])
                P.op("dve", "tensor_copy", [psTr], [qTb], out=qTb.t[:, h, 64:128], in_=psTr.t[:, 2 * h, 64:128])
            for c2 in range(2):
                P.op("dve", "scalar_tensor_tensor", [kk, cst, EB], [ktm], out=ktm.t[:, c2, :], in0=kk.t[:], scalar=cst.t[:, 544 + c2:545 + c2],
                     in1=EB.t[:, 1, :], op0=ALU.mult, op1=ALU.mult)
        for h in range(2):
            P.op("pe", "matmul", [qkT], [psAtt], out=psAtt.t[:, h, :], lhsT=qkT.t[:, 2 * h + 1, :], rhs=qkT.t[:, 2 * h, :], start=True, stop=True)
        for h in range(2):
            P.op("dve", "tensor_tensor", [psAtt, cst], [attm], out=attm.t[:, h, :], in0=psAtt.t[:, h, :], in1=tri, op=ALU.mult)
        if DEBUG_STOP == 5:
            P.barrier(); esA.close(); es0.close(); P.finish()
            return nc
        if not sample:
            for h in range(2):
                hs = slice(h * 128, (h + 1) * 128)
                P.op("pe", "matmul", [attm, vlin], [psO], out=psO.t[:, h, :], lhsT=attm.t[:, h, :], rhs=vlin.t[:, hs], start=True, stop=False)
                for c2 in range(2):
                    qm = qTa if c2 == 0 else qTb
                    P.op("pe", "matmul", [qm, Sb[h]], [psO], out=psO.t[:, h, :], lhsT=qm.t[:, h, :], rhs=Sb[h].t[:], start=False, stop=(c2 == 1))
                    P.op("pe", "matmul", [ktm, vlin], [psKV], out=psKV.t[:, h, :], lhsT=ktm.t[:, c2, hs], rhs=vlin.t[:, hs], start=True, stop=True)
                    P.op("dve", "tensor_tensor", [psKV, Sf[h]], [tmpS[h]], out=tmpS[h].t[:], in0=psKV.t[:, h, :], in1=Sf[h].t[:], op=ALU.add)
                    P.op("act", "activation", [tmpS[h], ebl], [Sf[h]], out=Sf[h].t[:], in_=tmpS[h].t[:], func=AF.Copy, scale=ebl.t[:, h, c2:c2 + 1])
                    P.op("dve", "tensor_scalar", [tmpS[h], ebl], [Sb[h]], out=Sb[h].t[:], in0=tmpS[h].t[:], scalar1=ebl.t[:, h, c2:c2 + 1], scalar2=None, op0=ALU.mult)
            if i == NT - 2:
                for h in range(2):
                    P.dma("sp", [Sf[h]], [], Sf[h], out=stp_out.t[h, :, :], in_=Sf[h].t[:])
        else:
            esS = contextlib.ExitStack()
            S0 = [P.sb("S0_%d" % k, [128, 2, 128], F32, esS) for k in range(4)]
            S0b = [P.sb("S0b_%d" % k, [128, 2, 128], BF16, esS) for k in range(4)]
            Sn = [P.sb("Sn_%d" % k, [128, 2, 128], F32, esS) for k in range(4)]
            vm = [P.sb("vm_%d" % k, [128, 256], BF16, esS) for k in range(4)]
            eblS = P.sb("eblS", [128, 2, 32], F32, esS)
            oiT = P.sb("oiT", [128, 2, 128], F32, esS)
            psOI = psB
            for h in range(2):
                P.op("act", "activation", [psBT], [eblS], out=eblS.t[:, h, :], in_=psBT.t[:, h, 3:128:4], func=AF.Exp)
            for b in range(32):
                k4 = b % 4
                P.dma("sp", [], [S0[k4]], S0[k4], out=S0[k4].t[:], in_=state_in.t[2 * b:2 * b + 2, :, :].rearrange("h k v -> k h v"))
                P.op("act", "copy", [S0[k4]], [S0b[k4]], out=S0b[k4].t[:], in_=S0[k4].t[:])
                P.op("dve", "tensor_scalar", [vlin, cst], [vm[k4]], out=vm[k4].t[:], in0=vlin.t[:], scalar1=bmask[:, b:b + 1], scalar2=None, op0=ALU.mult)
                for h in range(2):
                    hs = slice(h * 128, (h + 1) * 128)
                    P.op("pe", "matmul", [S0b[k4], qkT], [psOI], out=psOI.t[:, h * 128 + 4 * b:h * 128 + 4 * b + 4], lhsT=S0b[k4].t[:, h, :],
                         rhs=qkT.t[:, 2 * h, 4 * b:4 * b + 4], start=True, stop=True)
                    P.op("pe", "matmul", [qkt, vm[k4]], [psKV], out=psKV.t[:, h, :], lhsT=qkt.t[:, 1, hs], rhs=vm[k4].t[:, hs], start=True, stop=True)
                    P.op("dve", "tensor_tensor", [psKV, S0[k4]], [tmpS[h]], out=tmpS[h].t[:], in0=psKV.t[:, h, :], in1=S0[k4].t[:, h, :], op=ALU.add)
                    P.op("act", "activation", [tmpS[h], eblS], [Sn[k4]], out=Sn[k4].t[:, h, :], in_=tmpS[h].t[:], func=AF.Copy, scale=eblS.t[:, h, b:b + 1])
                P.dma("sp", [Sn[k4]], [], Sn[k4], out=sts_out.t[2 * b:2 * b + 2, :, :].rearrange("h k v -> k h v"), in_=Sn[k4].t[:])
            for h in range(2):
                P.op("act", "copy", [psOI], [oiT], out=oiT.t[:, h, :], in_=psOI.t[:, h * 128:(h + 1) * 128])
            for h in range(2):
                hs = slice(h * 128, (h + 1) * 128)
                P.op("pe", "matmul", [attm, vlin], [psO], out=psO.t[:, h, :], lhsT=attm.t[:, h, :], rhs=vlin.t[:, hs], start=True, stop=True)
            for h in range(2):
                P.op("pe", "transpose", [oiT, ident_f], [psAtt], out=psAtt.t[:, h, :], in_=oiT.t[:, h, :], identity=ident_f.t[:])
            esS.close()
        if DEBUG_STOP == 6:
            P.barrier(); esA.close(); es0.close(); P.finish()
            return nc
        if sample:
            osum = P.sb("osum", [128, 2, 128], F32, esA)
            for h in range(2):
                P.op("dve", "tensor_copy", [psAtt], [osum], out=osum.t[:, h, :], in_=psAtt.t[:, h, :])
                P.op("dve", "tensor_tensor", [psO, osum], [osum], out=osum.t[:, h, :], in0=psO.t[:, h, :], in1=osum.t[:, h, :], op=ALU.add)
            osrc = osum
        else:
            osrc = psO
        for h in range(2):
            P.op("act", "activation", [osrc], [junk, osq], out=junk.t[:, 0:128], in_=osrc.t[:, h, :], func=AF.Square, accum_out=osq.t[:, h:h + 1])
        P.op("act", "activation", [osq, epsb], [osq], out=osq.t[:], in_=osq.t[:], func=AF.Ln, scale=1.0 / 128, bias=epsb.t[:, 0:1])
        P.op("act", "activation", [osq], [osq], out=osq.t[:], in_=osq.t[:], func=AF.Exp, scale=-0.5)
        for h in range(2):
            hs = slice(h * 128, (h + 1) * 128)
            P.op("dve", "scalar_tensor_tensor", [osrc, osq, glsg], [olg], out=olg.t[:, hs], in0=osrc.t[:, h, :], scalar=osq.t[:, h:h + 1],
                 in1=glsg.t[:, hs], op0=ALU.mult, op1=ALU.mult)
        for h in range(2):
            P.op("pe", "transpose", [olg, ident_b], [psTr], out=psTr.t[:, h, :], in_=olg.t[:, h * 128:(h + 1) * 128], identity=ident_b.t[:])
        om = omT[i % 2]
        for h in range(2):
            P.op("act", "copy", [psTr], [om], out=om.t[:, h, :], in_=psTr.t[:, h, :])
        ibk, ibap = ib_tile(0, i)
        P.dma("sp", [om], [ibk], ibk, out=ibap, in_=om.t[:])
    P.barrier()
    esA.close()
    es0.close()


    if ENABLE_C:
        esC = contextlib.ExitStack()
        relb_s = P.sb("relb_s", [33, 4], F32, esC)
        oh_s = P.sb("oh_s", [33, TABW], F32, esC)
        rep = P.sb("rep", [33, 128], F32, esC)
        ones33 = P.sb("ones33", [33, 128], F32, esC)
        tabS = P.sb("tabS", [128, TABW], F32, esC)
        Bsk = [P.sb("Bsk%d" % h, [128, WB], F32, esC) for h in range(4)]
        pm_s = P.sb("pm_s", [128, 768], F32, esC)
        ksum = P.sb("ksum", [64, 64], F32, esC)
        kmTb = P.sb("kmTb", [64, 64], BF16, esC)
        g2 = P.sb("g2", [128, 16], F32, esC)
        m8 = P.sb("m8", [128, 8], F32, esC)
        selt = P.sb("selt", [128, 16], F32, esC)
        negpad = P.sb("negpad", [128, 80], BF16, esC)
        tmpL = [P.sb("tmpL%d" % k, [128, 512], F32, esC) for k in range(2)]
        PT = [P.sb("PT%d" % k, [128, 512], BF16, esC) for k in range(2)]
        oatt = [P.sb("oatt%d" % k, [128, 4, 256], BF16, esC) for k in range(2)]
        rec = P.sb("rec", [128, 4], F32, esC)
        omA = [P.sb("omA%d" % k, [128, 2, 128], BF16, esC) for k in range(2)]
        psS = [P.ps("psS%d" % k, [128, 512], F32, esC) for k in range(2)]
        psOa = [P.ps("psOa%d" % k, [128, 65], F32, esC) for k in range(4)]
        psG = P.ps("psG", [128, 512], F32, esC)
        psT2 = P.ps("psT2", [128, 2, 128], BF16, esC)
        psTab = psG
        P.dma("sp", [], [relb_s], relb_s, out=relb_s.t[0:32, :], in_=relb.t[:, :])
        P.op("pool", "memset", [], [relb_s], ap=relb_s.t[32:33, :], constant=-30000.0)
        P.op("pool", "memset", [], [ones33], ap=ones33.t[:], constant=1.0)
        P.op("pool", "memset", [], [negpad], ap=negpad.t[:], constant=0.0)
        P.dma("sp", [], [oh_s], oh_s, out=oh_s.t[:], in_=onehot.t[:, :])
        P.dma("sp", [], [pm_s], pm_s, out=pm_s.t[:], in_=pmask.t[:, :])
        CW = TABW // 4
        for h in range(4):
            P.op("dve", "tensor_scalar", [ones33, relb_s], [rep], out=rep.t[:], in0=ones33.t[:], scalar1=relb_s.t[:, h:h + 1], scalar2=None, op0=ALU.mult)
            for c4 in range(4):
                P.op("pe", "matmul", [rep, oh_s], [psTab], out=psTab.t[:, 0:CW], lhsT=rep.t[:], rhs=oh_s.t[:, c4 * CW:(c4 + 1) * CW], start=True, stop=True)
                P.op("act", "copy", [psTab], [tabS], out=tabS.t[:, c4 * CW:(c4 + 1) * CW], in_=psTab.t[:, 0:CW])
            P.dma("sp", [tabS], [btab], btab, out=btab.t[h, :, :], in_=tabS.t[:])
            skew = bass.AP(tensor=btab.t, offset=h * 128 * TABW + 127, ap=[[TABW - 1, 128], [1, WB]])
            P.dma("sp", [btab], [Bsk[h]], Bsk[h], out=Bsk[h].t[:], in_=skew)
        for h in range(4):
            for pg in range(16):
                j0 = min(2048 - 128 * pg, WB - 4)
                P.op("dve", "tensor_copy", [Bsk[h]], [Bsall], out=Bsall.t[:, h * 64 + pg * 4:h * 64 + pg * 4 + 4], in_=Bsk[h].t[:, j0:j0 + 4])
        ohn_s = P.sb("ohn_s", [33, 512], F32, esC)
        P.dma("sp", [], [ohn_s], ohn_s, out=ohn_s.t[:], in_=ohn.t[:, :])
        for h in range(4):
            for t4 in range(4):
                P.op("pe", "matmul", [ohn_s, relb_s], [psG], out=psG.t[:, h * 4 + t4:h * 4 + t4 + 1], lhsT=ohn_s.t[:, t4 * 128:(t4 + 1) * 128],
                     rhs=relb_s.t[:, h:h + 1], start=True, stop=True)
        P.op("act", "copy", [psG], [Bnn], out=Bnn.t[:], in_=psG.t[:, 0:16])
        P.op("pool", "memset", [], [ksum], ap=ksum.t[:], constant=0.0)
        for h in range(4):
            for n in range(NBLK):
                P.op("dve", "tensor_reduce", [kT], [ksum], out=ksum.t[0:64, h * 16 + n:h * 16 + n + 1], in_=kT.t[0:64, h, n * 256:(n + 1) * 256],
                     axis=AX.X, op=ALU.add)
        P.op("act", "activation", [ksum], [kmTb], out=kmTb.t[:], in_=ksum.t[:], func=AF.Copy, scale=1.0 / 256)
        for h in range(4):
            for qt in range(NP // 128):
                own = qt // 2
                cs = slice(qt * 128, (qt + 1) * 128)
                P.op("pe", "matmul", [qT, kmTb], [psG], out=psG.t[:, 0:16], lhsT=qT.t[0:64, h, cs], rhs=kmTb.t[0:64, h * 16:(h + 1) * 16], start=True, stop=True)
                P.op("dve", "tensor_tensor", [psG, pm_s], [g2], out=g2.t[:], in0=psG.t[:, 0:16], in1=pm_s.t[:, own * 48:own * 48 + 16], op=ALU.add)
                P.op("dve", "max", [g2], [m8], out=m8.t[:], in_=g2.t[:])
                P.op("dve", "tensor_scalar", [g2, m8], [selt], out=selt.t[:], in0=g2.t[:], scalar1=m8.t[:, 2:3], scalar2=None, op0=ALU.is_ge)
                P.op("dve", "tensor_tensor", [selt, pm_s], [selt], out=selt.t[:], in0=selt.t[:], in1=pm_s.t[:, own * 48 + 16:own * 48 + 32], op=ALU.mult)
                P.op("dve", "scalar_tensor_tensor", [selt, pm_s], [negpad], out=negpad.t[:, 64:80], in0=selt.t[:], scalar=32768.0,
                     in1=pm_s.t[:, own * 48 + 32:own * 48 + 48], op0=ALU.mult, op1=ALU.add)
                P.op("pe", "matmul", [negpad, ident_b], [psG], out=psG.t[0:80, 128:256], lhsT=negpad.t[:, 0:80], rhs=ident_b.t[:], start=True, stop=True)
                P.op("dve", "tensor_copy", [psG], [qT], out=qT.t[64:80, h, cs], in_=psG.t[64:80, 128:256])
        chunks = [(q0, min(512, NP - q0)) for q0 in range(0, NP, 512)]
        it = 0
        for (q0, ncols) in chunks:
            nsub = ncols // 128
            oa = oatt[(q0 // 512) % 2]
            for h in range(4):
                nkt = (q0 + ncols) // 128
                for kt in range(nkt):
                    k0 = kt * 128
                    f0 = max(0, k0 - q0)
                    pss = psS[it % 2]
                    pt = PT[it % 2]
                    it += 1
                    P.op("pe", "matmul", [kT, qT], [pss], out=pss.t[:, f0:ncols], lhsT=kT.t[0:80, h, k0:k0 + 128], rhs=qT.t[0:80, h, q0 + f0:q0 + ncols],
                         start=True, stop=True)
                    if q0 - k0 <= 896:
                        tl = tmpL[it % 2]
                        P.op("dve", "scalar_tensor_tensor", [pss, Bsk[h]], [tl], out=tl.t[:, f0:ncols], in0=pss.t[:, f0:ncols], scalar=0.125,
                             in1=Bsk[h].t[:, f0 + q0 - k0:ncols + q0 - k0], op0=ALU.mult, op1=ALU.add)
                        P.op("act", "activation", [tl], [pt], out=pt.t[:, f0:ncols], in_=tl.t[:, f0:ncols], func=AF.Exp)
                    else:
                        P.op("act", "activation", [pss, Bsk[h]], [pt], out=pt.t[:, f0:ncols], in_=pss.t[:, f0:ncols], func=AF.Exp, scale=0.125,
                             bias=Bsk[h].t[:, WB - 1:WB])
                    for qs in range(f0 // 128, nsub):
                        P.op("pe", "matmul", [pt, vaug], [psOa[qs]], out=psOa[qs].t[:], lhsT=pt.t[:, qs * 128:(qs + 1) * 128], rhs=vaug.t[:, kt, h, :],
                             start=(kt == 0), stop=(kt == q0 // 128 + qs))
                for qs in range(nsub):
                    tile_i = q0 // 128 + qs
                    pso = psOa[qs]
                    P.op("dve", "reciprocal", [pso], [rec], out=rec.t[:, qs:qs + 1], in_=pso.t[:, 64:65])
                    P.op("dve", "scalar_tensor_tensor", [pso, rec, sag], [oa], out=oa.t[:, qs, h * 64:(h + 1) * 64], in0=pso.t[:, 0:64],
                         scalar=rec.t[:, qs:qs + 1], in1=sag.t[:, tile_i, h * 64:(h + 1) * 64], op0=ALU.mult, op1=ALU.mult)
            for qs in range(nsub):
                tile_i = q0 // 128 + qs
                om = omA[tile_i % 2]
                for pr in range(2):
                    P.op("pe", "transpose", [oa, ident_b], [psT2], out=psT2.t[:, pr, :], in_=oa.t[:, qs, pr * 128:(pr + 1) * 128], identity=ident_b.t[:])
                for pr in range(2):
                    P.op("act", "copy", [psT2], [om], out=om.t[:, pr, :], in_=psT2.t[:, pr, :])
                ibk, ibap = ib_tile(256, tile_i)
                P.dma("sp", [om], [ibk], ibk, out=ibap, in_=om.t[:])
        P.barrier()
        esC.close()


    if ENABLE_C and ENABLE_D:
        esD = contextlib.ExitStack()
        pt_i = P.sb("pt_i", [128, 512], I32, esD)
        pt_f = P.sb("pt_f", [128, 512], F32, esD)
        idx = P.sb("idx", [128, 512], I32, esD)
        negb = P.sb("negb", [128, 32], F32, esD)
        onesb = P.sb("onesb", [128, 1], BF16, esD)
        kpg = [P.sb("kpg%d" % k, [128, 256], F32, esD) for k in range(4)]
        vpg = [P.sb("vpg%d" % k, [128, 256], F32, esD) for k in range(4)]
        kbp = [P.sb("kbp%d" % k, [128, 256], BF16, esD) for k in range(4)]
        vbuf = [P.sb("vbuf%d" % k, [128, 16, 256], BF16, esD) for k in range(2)]
        kTs = [P.sb("kTs%d" % k, [80, 16, 512], BF16, esD) for k in range(2)]
        ksS = P.sb("ksS", [64, 64], F32, esD)
        kmS = P.sb("kmS", [64, 32], F32, esD)
        kmSb = P.sb("kmSb", [64, 32], BF16, esD)
        gs = P.sb("gs", [4, 8], F32, esD)
        m8s = P.sb("m8s", [4, 8], F32, esD)
        sels = P.sb("sels", [4, 8], F32, esD)
        negs = P.sb("negs", [4, 80], BF16, esD)
        tls = P.sb("tls", [128, 256], F32, esD)
        PTs = P.sb("PTs", [128, 256], BF16, esD)
        tns = P.sb("tns", [128, 16], F32, esD)
        PTn = P.sb("PTn", [128, 16], BF16, esD)
        recs = P.sb("recs", [4, 4], F32, esD)
        onb = P.sb("onb", [4, 256], BF16, esD)
        sagT = P.sb("sagT", [128, 2, 128], BF16, esD)
        omS = P.sb("omS", [128, 2, 128], BF16, esD)
        psK = P.ps("psK", [128, 512], F32, esD)
        psKS = P.ps("psKS", [128, 512], F32, esD)
        psGd = P.ps("psGd", [128, 512], F32, esD)
        psSs = P.ps("psSs", [128, 512], F32, esD)
        psOs = P.ps("psOs", [128, 512], F32, esD)
        psOT = P.ps("psOT", [128, 2, 128], F32, esD)
        psTd = P.ps("psTd", [128, 2, 128], BF16, esD)
        P.dma("sp", [], [pt_i], pt_i, out=pt_i.t[:], in_=ptab.t[:, :])
        P.op("dve", "tensor_copy", [pt_i], [pt_f], out=pt_f.t[:], in_=pt_i.t[:])
        P.op("dve", "tensor_scalar", [pt_f, cst], [pt_f], out=pt_f.t[:], in0=pt_f.t[:], scalar1=128.0, scalar2=cst.t[:, 546:547], op0=ALU.mult, op1=ALU.add)
        P.op("dve", "tensor_copy", [pt_f], [idx], out=idx.t[:], in_=pt_f.t[:])
        P.op("dve", "tensor_scalar", [cst], [negb], out=negb.t[:], in0=bmask, scalar1=30000.0, scalar2=-30000.0, op0=ALU.mult, op1=ALU.add)
        P.op("pool", "memset", [], [onesb], ap=onesb.t[:], constant=1.0)
        P.op("pool", "memset", [], [negs], ap=negs.t[:], constant=0.0)
        for k in range(2):
            P.dma("sp", [], [kTs[k]], kTs[k], out=kTs[k].t[64:80, :, :], in_=kind2.t[:, :].rearrange("n (g c) -> n g c", g=16))
        for pr in range(2):
            P.op("pe", "transpose", [sag, ident_b], [psTd], out=psTd.t[:, pr, :], in_=sag.t[:, NT - 1, pr * 128:(pr + 1) * 128], identity=ident_b.t[:])
        for pr in range(2):
            P.op("act", "copy", [psTd], [sagT], out=sagT.t[:, pr, :], in_=psTd.t[:, pr, :])

        def gather(dst, src, col):
            def fn(eng, sem):
                eng.indirect_dma_start(out=dst.t[:, :], out_offset=None, in_=src.t[:, :],
                                       in_offset=bass.IndirectOffsetOnAxis(ap=idx.t[:, col:col + 1], axis=0)).then_inc(sem, 16)
                return 1
            P.dma("pool", [idx], [dst], dst, fn=fn)

        if DEBUG_STOP == 30:
            P.barrier(); esD.close(); P.finish()
            return nc
        for b in range(32):
            kt_b = kTs[b % 2]
            vb = vbuf[b % 2]
            qcols = slice(NP + 4 * b, NP + 4 * b + 4)
            for pg in range(16):
                kp, vp, kb = kpg[pg % 4], vpg[pg % 4], kbp[pg % 4]
                gather(kp, cache_k, b * 16 + pg)
                gather(vp, cache_v, b * 16 + pg)
                P.op("act", "copy", [kp], [kb], out=kb.t[:], in_=kp.t[:])
                P.op("dve", "tensor_copy", [vp], [vb], out=vb.t[:, pg, :], in_=vp.t[:])
                for h in range(4):
                    P.op("pe", "matmul", [kb, ident_b], [psK], out=psK.t[0:64, h * 128:(h + 1) * 128], lhsT=kb.t[:, h * 64:(h + 1) * 64], rhs=ident_b.t[:],
                         start=True, stop=True)
                P.op("dve" if pg % 2 == 0 else "act", "tensor_copy" if pg % 2 == 0 else "copy", [psK], [kt_b], out=kt_b.t[0:64, pg, :], in_=psK.t[0:64, :])
                for h in range(4):
                    P.op("pe", "matmul", [kb, onesb], [psKS], out=psKS.t[0:64, h * 16 + pg:h * 16 + pg + 1], lhsT=kb.t[:, h * 64:(h + 1) * 64], rhs=onesb.t[:],
                         start=True, stop=True)
            if DEBUG_STOP == 31 and b == 0:
                P.barrier(); esD.close(); P.finish()
                return nc
            P.op("act", "copy", [psKS], [ksS], out=ksS.t[:], in_=psKS.t[0:64, 0:64])
            P.op("dve", "tensor_tensor", [ksS], [kmS], out=kmS.t[:], in0=ksS.t[:, 0:64:2], in1=ksS.t[:, 1:64:2], op=ALU.add)
            P.op("act", "activation", [kmS], [kmSb], out=kmSb.t[:], in_=kmS.t[:], func=AF.Copy, scale=1.0 / 256)
            for h in range(4):
                P.op("pe", "matmul", [qT, kmSb], [psGd], out=psGd.t[0:4, h * 8:(h + 1) * 8], lhsT=qT.t[0:64, h, qcols], rhs=kmSb.t[0:64, h * 8:(h + 1) * 8],
                     start=True, stop=True)
                P.op("dve", "tensor_copy", [psGd], [gs], out=gs.t[:], in_=psGd.t[0:4, h * 8:(h + 1) * 8])
                P.op("dve", "max", [gs], [m8s], out=m8s.t[:], in_=gs.t[:])
                P.op("dve", "tensor_scalar", [gs, m8s], [sels], out=sels.t[:], in0=gs.t[:], scalar1=m8s.t[:, 2:3], scalar2=None, op0=ALU.is_ge)
                P.op("dve", "tensor_scalar", [sels], [negs], out=negs.t[:, 64:72], in0=sels.t[:], scalar1=32768.0, scalar2=-32768.0, op0=ALU.mult, op1=ALU.add)
                P.op("pe", "matmul", [negs, ident_b], [psGd], out=psGd.t[0:80, 64:68], lhsT=negs.t[0:4, 0:80], rhs=ident_b.t[0:4, 0:4], start=True, stop=True)
                P.op("dve", "tensor_copy", [psGd], [qT], out=qT.t[64:80, h, qcols], in_=psGd.t[64:80, 64:68])
            if DEBUG_STOP == 32 and b == 0:
                P.barrier(); esD.close(); P.finish()
                return nc
            for h in range(4):
                for pg in range(16):
                    P.op("pe", "matmul", [kt_b, qT], [psSs], out=psSs.t[:, h * 64 + pg * 4:h * 64 + pg * 4 + 4], lhsT=kt_b.t[0:80, pg, h * 128:(h + 1) * 128],
                         rhs=qT.t[0:80, h, qcols], start=True, stop=True)
                P.op("pe", "matmul", [kT, qT], [psSs], out=psSs.t[:, 256 + h * 4:256 + h * 4 + 4], lhsT=kT.t[0:80, h, NP:NTOK], rhs=qT.t[0:80, h, qcols],
                     start=True, stop=True)
            P.op("dve", "scalar_tensor_tensor", [psSs, Bsall], [tls], out=tls.t[:], in0=psSs.t[:, 0:256], scalar=0.125, in1=Bsall.t[:], op0=ALU.mult, op1=ALU.add)
            P.op("act", "activation", [tls], [PTs], out=PTs.t[:], in_=tls.t[:], func=AF.Exp)
            P.op("dve", "scalar_tensor_tensor", [psSs, Bnn], [tns], out=tns.t[:], in0=psSs.t[:, 256:272], scalar=0.125, in1=Bnn.t[:], op0=ALU.mult, op1=ALU.add)
            P.op("dve", "tensor_scalar", [tns, negb], [tns], out=tns.t[:], in0=tns.t[:], scalar1=negb.t[:, b:b + 1], scalar2=None, op0=ALU.add)
            P.op("act", "activation", [tns], [PTn], out=PTn.t[:], in_=tns.t[:], func=AF.Exp)
            if DEBUG_STOP == 33 and b == 0:
                P.barrier(); esD.close(); P.finish()
                return nc
            for h in range(4):
                for pg in range(16):
                    P.op("pe", "matmul", [PTs, vb], [psOs], out=psOs.t[0:4, h * 65:h * 65 + 64], lhsT=PTs.t[:, h * 64 + pg * 4:h * 64 + pg * 4 + 4],
                         rhs=vb.t[:, pg, h * 64:(h + 1) * 64], start=(pg == 0), stop=False)
                P.op("pe", "matmul", [PTn, vaug], [psOs], out=psOs.t[0:4, h * 65:h * 65 + 64], lhsT=PTn.t[:, h * 4:h * 4 + 4], rhs=vaug.t[:, NT - 1, h, 0:64],
                     start=False, stop=True)
                for pg in range(16):
                    P.op("pe", "matmul", [PTs, onesb], [psOs], out=psOs.t[0:4, h * 65 + 64:h * 65 + 65], lhsT=PTs.t[:, h * 64 + pg * 4:h * 64 + pg * 4 + 4],
                         rhs=onesb.t[:], start=(pg == 0), stop=False)
                P.op("pe", "matmul", [PTn, vaug], [psOs], out=psOs.t[0:4, h * 65 + 64:h * 65 + 65], lhsT=PTn.t[:, h * 4:h * 4 + 4], rhs=vaug.t[:, NT - 1, h, 64:65],
                     start=False, stop=True)
            for h in range(4):
                P.op("dve", "reciprocal", [psOs], [recs], out=recs.t[:, h:h + 1], in_=psOs.t[0:4, h * 65 + 64:h * 65 + 65])
                P.op("dve", "tensor_scalar", [psOs, recs], [onb], out=onb.t[:, h * 64:(h + 1) * 64], in0=psOs.t[0:4, h * 65:h * 65 + 64], scalar1=recs.t[:, h:h + 1],
                     scalar2=None, op0=ALU.mult)
            if DEBUG_STOP == 36 and b == 0:
                P.barrier(); esD.close(); P.finish()
                return nc
            for pr in range(2):
                P.op("pe", "matmul", [onb, ident_b], [psOT], out=psOT.t[:, pr, 4 * b:4 * b + 4], lhsT=onb.t[0:4, pr * 128:(pr + 1) * 128], rhs=ident_b.t[0:4, 0:4],
                     start=True, stop=True)
            if DEBUG_STOP == 34 and b == 0:
                P.barrier(); esD.close(); P.finish()
                return nc
            if DEBUG_STOP == 35 and b == 3:
                P.barrier(); esD.close(); P.finish()
                return nc
        for pr in range(2):
            P.op("dve", "tensor_tensor", [psOT, sagT], [omS], out=omS.t[:, pr, :], in0=psOT.t[:, pr, :], in1=sagT.t[:, pr, :], op=ALU.mult)
        ibk, ibap = ib_tile(256, NT - 1)
        P.dma("sp", [omS], [ibk], ibk, out=ibap, in_=omS.t[:])
        P.barrier()
        esD.close()

    if not ENABLE_C:
        esZ = contextlib.ExitStack()
        zt = P.sb("zt", [128, 2, 128], BF16, esZ)
        P.op("pool", "memset", [], [zt], ap=zt.t[:], constant=0.0)
        for i in range(NT):
            ibk, ibap = ib_tile(256, i)
            P.dma("sp", [zt], [ibk], ibk, out=ibap, in_=zt.t[:])
        P.barrier()
        esZ.close()

    if ENABLE_C and not ENABLE_D:
        esZ = contextlib.ExitStack()
        zt = P.sb("zt2", [128, 2, 128], BF16, esZ)
        P.op("pool", "memset", [], [zt], ap=zt.t[:], constant=0.0)
        ibk, ibap = ib_tile(256, NT - 1)
        P.dma("sp", [zt], [ibk], ibk, out=ibap, in_=zt.t[:])
        P.barrier()
        esZ.close()

    P.barrier()
    cc_sem = P.es.enter_context(nc.semaphore("cc_sem"))
    for k in range(NCH):
        nc.gpsimd.collective_compute("AllGather", ALU.bypass, replica_groups=[list(g) for g in GROUPS],
                                     ins=[ibs[k].t.ap().opt()], outs=[obs[k].t.ap().opt()]).then_inc(cc_sem)
        nc.gpsimd.wait_ge(cc_sem, k + 1)
    for e in P.eng:
        P.eng[e].wait_ge(cc_sem, NCH)
    esE = contextlib.ExitStack()
    Wo = P.sb("Wo", [128, 8, 1024], BF16, esE)
    wst2 = [P.sb("wst2_%d" % i, [128, 1024], F32, esE) for i in range(2)]
    gpost_s = P.sb("gpost_s", [128, 1024], F32, esE)
    obt = [P.sb("obt%d" % i, [128, 8, 128], BF16, esE) for i in range(2)]
    xt2 = [P.sb("xt2_%d" % i, [128, 1024], F32, esE) for i in range(2)]
    yt = [P.sb("yt%d" % i, [128, 1024], F32, esE) for i in range(2)]
    junk2 = P.sb("junk2", [128, 512], BF16, esE)
    ssq = P.sb("ssq", [128, 4], F32, esE)
    psY = [P.ps("psY%d" % i, [128, 512], F32, esE) for i in range(2)]
    P.dma("sp", [], [gpost_s], gpost_s, out=gpost_s.t[:], in_=gpost.t[:, :])
    for j in range(8):
        w = wst2[j % 2]
        P.dma("sp", [], [w], w, out=w.t[:], in_=w_out.t[j * 128:(j + 1) * 128, :])
        if j % 2 == 0:
            P.op("dve", "tensor_copy", [w], [Wo], out=Wo.t[:, j, :], in_=w.t[:])
        else:
            P.op("act", "copy", [w], [Wo], out=Wo.t[:, j, :], in_=w.t[:])

    def load_tail(i):
        obk, c0 = obs[i // CHT], (i % CHT) * 128
        P.dma("sp", [obk], [obt[i % 2]], obt[i % 2], out=obt[i % 2].t[:], in_=obk.t[:, c0:c0 + 128].rearrange("(j p) t -> p j t", p=128))
        P.dma("sp", [], [xt2[i % 2]], xt2[i % 2], out=xt2[i % 2].t[:], in_=xall.t[i * 128:(i + 1) * 128, :])

    load_tail(0)
    for i in range(NT):
        if i + 1 < NT:
            load_tail(i + 1)
        o_t, x_t, y_t = obt[i % 2], xt2[i % 2], yt[i % 2]
        for cc in range(2):
            for j in range(8):
                P.op("pe", "matmul", [o_t, Wo], [psY[cc]], out=psY[cc].t[:], lhsT=o_t.t[:, j, :], rhs=Wo.t[:, j, cc * 512:(cc + 1) * 512],
                     start=(j == 0), stop=(j == 7))
            P.op("act", "activation", [psY[cc]], [junk2, ssq], out=junk2.t[:], in_=psY[cc].t[:], func=AF.Square, accum_out=ssq.t[:, cc:cc + 1])
        P.op("dve", "tensor_tensor", [ssq], [ssq], out=ssq.t[:, 2:3], in0=ssq.t[:, 0:1], in1=ssq.t[:, 1:2], op=ALU.add)
        P.op("act", "activation", [ssq, epsb], [ssq], out=ssq.t[:, 3:4], in_=ssq.t[:, 2:3], func=AF.Ln, scale=1.0 / 1024, bias=epsb.t[:, 0:1])
        P.op("act", "activation", [ssq], [ssq], out=ssq.t[:, 2:3], in_=ssq.t[:, 3:4], func=AF.Exp, scale=-0.5)
        for cc in range(2):
            cs = slice(cc * 512, (cc + 1) * 512)
            P.op("dve", "scalar_tensor_tensor", [psY[cc], ssq, gpost_s], [y_t], out=y_t.t[:, cs], in0=psY[cc].t[:], scalar=ssq.t[:, 2:3],
                 in1=gpost_s.t[:, cs], op0=ALU.mult, op1=ALU.mult)
        P.op("dve", "tensor_tensor", [y_t, x_t], [y_t], out=y_t.t[:], in0=y_t.t[:], in1=x_t.t[:], op=ALU.add)
        P.dma("sp", [y_t], [], y_t, out=y_out.t[i * 128:(i + 1) * 128, :], in_=y_t.t[:])
    P.barrier()
    esE.close()

    P.finish()
    return nc


def _host_consts():
    c = np.zeros((128, 1024), np.float32)
    s = np.arange(128)[:, None]
    t = np.arange(128)[None, :]
    c[:, 0:128] = ((s // 64 == t // 64) & (s <= t)).astype(np.float32)
    c[:, 128:256] = ((s // 4 == t // 4) & (s <= t)).astype(np.float32)
    c[:, 256:384] = np.eye(128, dtype=np.float32)
    c[:, 512:544] = (s // 4 == np.arange(32)[None, :]).astype(np.float32)
    c[:, 544] = (np.arange(128) < 64).astype(np.float32)
    c[:, 545] = (np.arange(128) >= 64).astype(np.float32)
    c[:, 546] = np.arange(128).astype(np.float32)
    return c


def host_const_inputs():
    NP = NTOK - 128
    kind = np.zeros((16, NTOK), np.float32)
    for n in range(NP // 256):
        kind[n, n * 256:(n + 1) * 256] = 1.0
    kind[8, NP:] = 1.0
    u = np.arange(TABW)
    d = u - 127
    onehot = np.zeros((33, TABW), np.float32)
    bk = t5_bucket_np(d)
    onehot[bk[d >= 0], u[d >= 0]] = 1.0
    onehot[32, u[d < 0]] = 1.0
    pm = np.zeros((16, 48), np.float32)
    n = np.arange(16)
    for o in range(16):
        pm[o, 0:16] = np.where(n < o, 0.0, -1e30)
        pm[o, 16:32] = (n < o).astype(np.float32)
        pm[o, 32:48] = np.where(n == o, 0.0, -32768.0)
    pmask = np.ascontiguousarray(np.broadcast_to(pm.reshape(1, 768), (128, 768)))
    ohn = np.zeros((33, 4, 128), np.float32)
    for t4 in range(4):
        for p in range(128):
            dd = t4 - (p % 4)
            if dd < 0:
                ohn[32, t4, p] = 1.0
            else:
                ohn[int(t5_bucket_np(np.array([dd]))[0]), t4, p] = 1.0
    kind2 = np.zeros((16, 16, 512), np.float32)
    for pg in range(16):
        kind2[pg // 2, pg, :] = 1.0
    return {"kind": kind.astype(ml_dtypes.bfloat16), "onehot": onehot, "pmask": pmask,
            "ohn": ohn.reshape(33, 512), "kind2": kind2.reshape(16, 16 * 512).astype(ml_dtypes.bfloat16)}


def kernel(x_prompt, x_sample, cache_k, cache_v, state_hgrn, page_table, w_in, w_out,
           norm_pre, norm_post, norm_lin_out, lin_lower_bound, rel_bias):
    x_prompt = np.asarray(x_prompt, np.float32)
    x_sample = np.asarray(x_sample, np.float32)
    cache_k = np.asarray(cache_k, np.float32)
    cache_v = np.asarray(cache_v, np.float32)
    state_hgrn = np.asarray(state_hgrn, np.float32)
    page_table = np.asarray(page_table, np.int32)
    w_in = np.asarray(w_in, np.float32)[0]
    w_out = np.asarray(w_out, np.float32)[0]
    norm_pre = np.asarray(norm_pre, np.float32)[0]
    norm_post = np.asarray(norm_post, np.float32)[0]
    norm_lin_out = np.asarray(norm_lin_out, np.float32)[0]
    llb = np.asarray(lin_lower_bound, np.float32)
    rel_bias = np.asarray(rel_bias, np.float32)

    nc = build_program()
    consts = _host_consts()
    hc = host_const_inputs()
    in_maps = []
    for c in range(8):
        b, hh = c // 2, c % 2
        lh = [2 * hh, 2 * hh + 1]
        ah = list(range(4 * hh, 4 * hh + 4))
        lcols = lambda base: np.concatenate([np.arange(base + H * 128, base + (H + 1) * 128) for H in lh])
        acols = lambda base: np.concatenate([np.arange(base + A * 64, base + (A + 1) * 64) for A in ah])
        cols = np.concatenate([lcols(0), lcols(512), lcols(1536), acols(3584), lcols(1024), acols(3072), acols(2048), acols(2560)])
        xall = np.concatenate([x_prompt[b], x_sample[32 * b:32 * b + 32].reshape(128, 1024)], axis=0)
        lbc = np.concatenate([llb[0, lcols(0)], llb[1, lcols(0)]])
        wrows = np.concatenate([np.arange(0, 256), 512 + np.arange(0, 256), 256 + np.arange(0, 256), 768 + np.arange(0, 256)])
        m = {
            "xall": np.ascontiguousarray(xall),
            "w_in": np.ascontiguousarray(w_in[:, cols]),
            "w_out": np.ascontiguousarray(w_out[wrows]),
            "gpre": np.ascontiguousarray(norm_pre.reshape(8, 128).T),
            "gpost": np.ascontiguousarray(np.broadcast_to(norm_post, (128, 1024))),
            "glin": np.ascontiguousarray(np.broadcast_to(np.tile(norm_lin_out, 2), (128, 256))),
            "llb": np.ascontiguousarray(np.broadcast_to(lbc, (128, 512))),
            "relb": np.ascontiguousarray(rel_bias[:, ah]),
            "onehot": hc["onehot"],
            "consts": consts,
            "kind": hc["kind"],
            "pmask": hc["pmask"],
            "ohn": hc["ohn"],
            "kind2": hc["kind2"],
            "cache_k": np.ascontiguousarray(cache_k[0][:, :, ah, :]).reshape(2560 * 128, 256),
            "cache_v": np.ascontiguousarray(cache_v[0][:, :, ah, :]).reshape(2560 * 128, 256),
            "ptab": np.ascontiguousarray(np.broadcast_to(page_table[32 * b:32 * b + 32].reshape(512), (128, 512))),
            "state_in": np.ascontiguousarray(state_hgrn[0, 32 * b:32 * b + 32][:, lh]).reshape(64, 128, 128),
        }
        in_maps.append(m)
    res = run_bass_kernel_spmd(nc, in_maps, core_ids=list(range(8))).results

    y_p = np.zeros((4, 4096, 1024), np.float32)
    y_s = np.zeros((128, 4, 1024), np.float32)
    k_p = np.zeros((1, 4, 4096, 8, 64), np.float32)
    v_p = np.zeros((1, 4, 4096, 8, 64), np.float32)
    st_p = np.zeros((1, 4, 4, 128, 128), np.float32)
    k_s = np.zeros((1, 128, 4, 8, 64), np.float32)
    v_s = np.zeros((1, 128, 4, 8, 64), np.float32)
    st_s = np.zeros((1, 128, 4, 128, 128), np.float32)
    for c in range(8):
        b, hh = c // 2, c % 2
        r = res[c]
        kn = r["k_new"].reshape(NTOK, 4, 64)
        vn = r["v_new"].reshape(NTOK, 4, 64)
        k_p[0, b, :, 4 * hh:4 * hh + 4] = kn[:4096]
        v_p[0, b, :, 4 * hh:4 * hh + 4] = vn[:4096]
        k_s[0, 32 * b:32 * b + 32, :, 4 * hh:4 * hh + 4] = kn[4096:].reshape(32, 4, 4, 64)
        v_s[0, 32 * b:32 * b + 32, :, 4 * hh:4 * hh + 4] = vn[4096:].reshape(32, 4, 4, 64)
        st_p[0, b, 2 * hh:2 * hh + 2] = r["st_p"]
        st_s[0, 32 * b:32 * b + 32, 2 * hh:2 * hh + 2] = r["st_s"].reshape(32, 2, 128, 128)
        y = r["y"]
        half = slice(2048 * hh, 2048 * hh + 2048)
        y_p[b, half] = y[:4096][half]
        y_s[32 * b + 16 * hh:32 * b + 16 * hh + 16] = y[4096:].reshape(32, 4, 1024)[16 * hh:16 * hh + 16]
    return (y_p, y_s, k_p, v_p, st_p, k_s, v_s, st_s)
```

```python
import contextlib
import math
import numpy as np
import ml_dtypes
import concourse.bass as bass
import concourse.mybir as mybir
from concourse.bass_utils import run_bass_kernel_spmd

F32 = mybir.dt.float32
BF16 = mybir.dt.bfloat16
I32 = mybir.dt.int32
AF = mybir.ActivationFunctionType
ALU = mybir.AluOpType
AX = mybir.AxisListType

NT = 33
NTOK = NT * 128
EPS = 1e-6
NEG = -32768.0
WB = 1792
JOFF = 511
TABW = WB + 128
NPHYS = 2560
DEBUG_STOP = 0
ENABLE_QK = True
ENABLE_C = True
ENABLE_D = True
GROUPS = [[0, 1], [2, 3], [4, 5], [6, 7]]


class Buf:
    def __init__(self, name, t):
        self.name = name
        self.t = t
        self.w = []
        self.r = []
        self.dsem = None
        self.root = self
        self.excl = False

    def view(self, ap):
        v = Buf(self.name + "_v", ap)
        v.root = self.root
        return v


class Prog:
    def __init__(self, nc):
        self.nc = nc
        self.es = contextlib.ExitStack()
        self.eng = {"pe": nc.tensor, "act": nc.scalar, "dve": nc.vector, "pool": nc.gpsimd, "sp": nc.sync}
        self.sem = {k: self.es.enter_context(nc.semaphore("s_" + k)) for k in self.eng}
        self.cnt = {k: 0 for k in self.eng}
        self.known = {k: {} for k in self.eng}
        self.dsems = []
        self.dcnt = {}
        self.nbuf = 0

    def sb(self, name, shape, dt, es=None):
        t = (es or self.es).enter_context(self.nc.sbuf_tensor(name, shape, dt))
        return Buf(name, t)

    def ps(self, name, shape, dt, es=None):
        t = (es or self.es).enter_context(self.nc.psum_tensor(name, shape, dt))
        b = Buf(name, t)
        b.excl = True
        return b

    def dram(self, name, shape, dt, kind=None):
        if kind is None:
            t = self.nc.dram_tensor(name, shape, dt)
        else:
            t = self.nc.dram_tensor(name, shape, dt, kind=kind)
        return Buf(name, t)

    def _dsem(self, buf):
        buf = buf.root
        if buf.dsem is None:
            s = self.es.enter_context(self.nc.semaphore("d%d" % len(self.dsems)))
            self.dsems.append(s)
            self.dcnt[id(s)] = 0
            buf.dsem = s
        return buf.dsem

    def _waits(self, e, reads, writes):
        need = {}

        def add(ev):
            s, v = ev
            if need.get(id(s), (s, 0))[1] < v:
                need[id(s)] = (s, v)
        for b in reads:
            for ev in b.root.w:
                add(ev)
        for b in writes:
            for ev in b.root.w:
                add(ev)
            for ev in b.root.r:
                add(ev)
        kn = self.known[e]
        eng = self.eng[e]
        for k, (s, v) in need.items():
            if e == "pe" and s is self.sem["pe"]:
                continue
            if kn.get(k, 0) >= v:
                continue
            eng.wait_ge(s, v)
            kn[k] = v

    def _record(self, ev, reads, writes):
        for b in reads:
            b = b.root
            b.r = [x for x in b.r if x[0] is not ev[0]] + [ev]
        for b in writes:
            b = b.root
            b.w = [ev]
            b.r = []

    def op(self, e, name, reads, writes, **kw):
        writes = list(writes) + [b for b in reads if b.root.excl]
        self._waits(e, reads, writes)
        ins = getattr(self.eng[e], name)(**kw)
        self.cnt[e] += 1
        ins.then_inc(self.sem[e], 1)
        self._record((self.sem[e], self.cnt[e]), reads, writes)

    def dma(self, e, reads, writes, sem_of, n=1, fn=None, **kw):
        self._waits(e, reads, writes)
        s = self._dsem(sem_of)
        if fn is None:
            self.eng[e].dma_start(**kw).then_inc(s, 16)
        else:
            n = fn(self.eng[e], s)
        self.dcnt[id(s)] += 16 * n
        self._record((s, self.dcnt[id(s)]), reads, writes)

    def barrier(self):
        evs = [(self.sem[k], self.cnt[k]) for k in self.eng] + [(s, self.dcnt[id(s)]) for s in self.dsems]
        for e in self.eng:
            kn = self.known[e]
            for s, v in evs:
                if v == 0 or kn.get(id(s), 0) >= v:
                    continue
                if s is self.sem[e] and e == "pe":
                    continue
                self.eng[e].wait_ge(s, v)
                kn[id(s)] = v

    def finish(self):
        self.barrier()
        self.es.close()


def t5_bucket_np(n):
    n = np.maximum(n, 0).astype(np.int64)
    nf = np.maximum(n, 1).astype(np.float32)
    large = 16 + (np.log(nf / np.float32(16)) / np.float32(math.log(1024 / 16)) * np.float32(16)).astype(np.int32)
    large = np.minimum(large, 31)
    return np.where(n < 16, n, large)


def build_program(phases=("A", "C", "D", "E")):
    nc = bass.Bass("TRN2", target_bir_lowering=False)
    P = Prog(nc)
    din = lambda name, shape, dt=F32: P.dram(name, shape, dt, kind="ExternalInput")
    dout = lambda name, shape, dt=F32: P.dram(name, shape, dt, kind="ExternalOutput")
    xall = din("xall", [NTOK, 1024])
    w_in = din("w_in", [1024, 2048])
    w_out = din("w_out", [1024, 1024])
    gpre = din("gpre", [128, 8])
    gpost = din("gpost", [128, 1024])
    glin = din("glin", [128, 256])
    llb = din("llb", [128, 512])
    relb = din("relb", [32, 4])
    NP = NTOK - 128
    NBLK = NP // 256
    onehot = din("onehot", [33, TABW])
    consts = din("consts", [128, 1024])
    kind_c = din("kind", [16, NTOK], BF16)
    pmask = din("pmask", [128, 3 * 16 * 16])
    cache_k = din("cache_k", [NPHYS * 128, 256])
    cache_v = din("cache_v", [NPHYS * 128, 256])
    ptab = din("ptab", [128, 512], I32)
    ohn = din("ohn", [33, 4 * 128])
    kind2 = din("kind2", [16, 16 * 512], BF16)
    state_in = din("state_in", [64, 128, 128])
    y_out = dout("y", [NTOK, 1024])
    k_out = dout("k_new", [NTOK, 256])
    v_out = dout("v_new", [NTOK, 256])
    stp_out = dout("st_p", [2, 128, 128])
    sts_out = dout("st_s", [64, 128, 128])
    CHT = 11
    NCH = (NT + CHT - 1) // CHT
    chw = [min(CHT, NT - CHT * k) * 128 for k in range(NCH)]
    ibs = [P.dram("ib%d" % k, [512, chw[k]], BF16) for k in range(NCH)]
    obs = [P.dram("ob%d" % k, [1024, chw[k]], BF16) for k in range(NCH)]

    def ib_tile(r0, i):
        k, c = i // CHT, (i % CHT) * 128
        return ibs[k], ibs[k].t[r0:r0 + 256, c:c + 128].rearrange("(h p) t -> p h t", h=2)
    btab = P.dram("btab", [4, 128, TABW], F32)

    ident_b = P.sb("ident_b", [128, 128], BF16)
    ident_f = P.sb("ident_f", [128, 128], F32)
    cst = P.sb("cst", [128, 1024], F32)
    tri_p = cst.t[:, 0:128]
    tri_s = cst.t[:, 128:256]
    bmask = cst.t[:, 512:544]
    qT = P.sb("qT", [80, 4, NTOK], BF16)
    kT = P.sb("kT", [80, 4, NTOK], BF16)
    vaug = P.sb("vaug", [128, NT, 4, 65], BF16)
    sag = P.sb("sag", [128, NT, 256], BF16)
    lbt = P.sb("lbt", [128, 2, 256], F32)
    glin_s = P.sb("glin_s", [128, 256], F32)
    gpre_s = P.sb("gpre_s", [128, 8], F32)
    epsb = P.sb("epsb", [128, 2], F32)
    Bsall = P.sb("Bsall", [128, 256], F32)
    Bnn = P.sb("Bnn", [128, 16], F32)

    P.dma("sp", [], [cst], cst, out=cst.t[:], in_=consts.t[:, :])
    P.dma("sp", [], [glin_s], glin_s, out=glin_s.t[:], in_=glin.t[:, :])
    P.dma("sp", [], [gpre_s], gpre_s, out=gpre_s.t[:], in_=gpre.t[:, :])
    P.dma("sp", [], [kT], kT, out=kT.t[64:80, 0, :], in_=kind_c.t[:, :])
    P.dma("sp", [], [kT], kT, out=kT.t[64:80, 1, :], in_=kind_c.t[:, :])
    P.dma("sp", [], [kT], kT, out=kT.t[64:80, 2, :], in_=kind_c.t[:, :])
    P.dma("sp", [], [kT], kT, out=kT.t[64:80, 3, :], in_=kind_c.t[:, :])
    P.op("pool", "memset", [], [epsb], ap=epsb.t[:, 0:1], constant=EPS)
    P.op("pool", "memset", [], [epsb], ap=epsb.t[:, 1:2], constant=1.0)
    P.op("pool", "memset", [], [vaug], ap=vaug.t[:, :, :, 64:65], constant=1.0)
    if DEBUG_STOP == 10:
        P.finish()
        return nc
    P.op("dve", "tensor_copy", [cst], [ident_f], out=ident_f.t[:], in_=cst.t[:, 256:384])
    P.op("dve", "tensor_copy", [cst], [ident_b], out=ident_b.t[:], in_=cst.t[:, 256:384])
    es0 = contextlib.ExitStack()
    llb_s = P.sb("llb_s", [128, 512], F32, es0)
    P.dma("sp", [], [llb_s], llb_s, out=llb_s.t[:], in_=llb.t[:, :])
    P.op("dve", "tensor_tensor", [llb_s], [lbt], out=lbt.t[:, 1, :], in0=llb_s.t[:, 256:512], in1=llb_s.t[:, 0:256], op=ALU.subtract)
    P.op("act", "activation", [lbt], [lbt], out=lbt.t[:, 1, :], in_=lbt.t[:, 1, :], func=AF.Exp)
    P.op("dve", "tensor_scalar", [lbt], [lbt], out=lbt.t[:, 1, :], in0=lbt.t[:, 1, :], scalar1=1.0, scalar2=None, op0=ALU.add)
    P.op("dve", "reciprocal", [lbt], [lbt], out=lbt.t[:, 0, :], in_=lbt.t[:, 1, :])
    P.op("dve", "tensor_scalar", [lbt], [lbt], out=lbt.t[:, 1, :], in0=lbt.t[:, 0, :], scalar1=-1.0, scalar2=1.0, op0=ALU.mult, op1=ALU.add)

    if DEBUG_STOP == 11:
        P.barrier(); es0.close(); P.finish()
        return nc
    esA = contextlib.ExitStack()
    Wb = P.sb("Wb", [128, 8, 2048], BF16, esA)
    esW = contextlib.ExitStack()
    wst = [P.sb("wst%d" % i, [128, 2048], F32, esW) for i in range(2)]
    for j in range(8):
        w = wst[j % 2]
        P.dma("sp", [], [w], w, out=w.t[:], in_=w_in.t[j * 128:(j + 1) * 128, :])
        if j % 2 == 0:
            P.op("dve", "tensor_scalar", [w, gpre_s], [Wb], out=Wb.t[:, j, :], in0=w.t[:],
                 scalar1=gpre_s.t[:, j:j + 1], scalar2=None, op0=ALU.mult)
        else:
            P.op("act", "activation", [w, gpre_s], [Wb], out=Wb.t[:, j, :], in_=w.t[:], func=AF.Copy, scale=gpre_s.t[:, j:j + 1])
    P.barrier()
    esW.close()
    if DEBUG_STOP == 1:
        P.barrier(); esA.close(); es0.close(); P.finish()
        return nc

    xt = [P.sb("xt%d" % i, [128, 1024], F32, esA) for i in range(2)]
    junk = P.sb("junk", [128, 1024], F32, esA)
    hb = P.sb("hb", [128, 1024], BF16, esA)
    hT = P.sb("hT", [128, 8, 128], BF16, esA)
    ss = P.sb("ss", [128, 8], F32, esA)
    E0 = P.sb("E0", [128, 512], F32, esA)
    E1 = P.sb("E1", [128, 512], F32, esA)
    qf = P.sb("qf", [128, 256], F32, esA)
    ff = P.sb("ff", [128, 256], F32, esA)
    gg = P.sb("gg", [128, 256], F32, esA)
    kk = P.sb("kk", [128, 256], F32, esA)
    sgl = P.sb("sgl", [128, 256], F32, esA)
    vlin = P.sb("vlin", [128, 256], BF16, esA)
    vo = [P.sb("vo%d" % i, [128, 256], F32, esA) for i in range(2)]
    ko = [P.sb("ko%d" % i, [128, 256], F32, esA) for i in range(2)]
    qkb = P.sb("qkb", [128, 512], BF16, esA)
    EB = P.sb("EB", [128, 2, 256], F32, esA)
    qkt = P.sb("qkt", [128, 2, 256], BF16, esA)
    qkT = P.sb("qkT", [128, 4, 128], BF16, esA)
    attm = P.sb("attm", [128, 2, 128], BF16, esA)
    qTa = P.sb("qTa", [128, 2, 128], BF16, esA)
    qTb = P.sb("qTb", [128, 2, 128], BF16, esA)
    ktm = P.sb("ktm", [128, 2, 256], BF16, esA)
    ebl = P.sb("ebl", [128, 2, 2], F32, esA)
    Sf = [P.sb("Sf%d" % h, [128, 128], F32, esA) for h in range(2)]
    Sb = [P.sb("Sb%d" % h, [128, 128], BF16, esA) for h in range(2)]
    tmpS = [P.sb("tmpS%d" % h, [128, 128], F32, esA) for h in range(2)]
    osq = P.sb("osq", [128, 2], F32, esA)
    olg = P.sb("olg", [128, 256], BF16, esA)
    glsg = P.sb("glsg", [128, 256], F32, esA)
    omT = [P.sb("omT%d" % i, [128, 2, 128], BF16, esA) for i in range(2)]
    psT = P.ps("psT", [128, 8, 128], BF16, esA)
    psZ = [P.ps("psZ%d" % i, [128, 512], F32, esA) for i in range(4)]
    bankF = P.ps("bankF", [128, 512], F32, esA)
    bankG = P.ps("bankG", [128, 512], F32, esA)
    bankH = P.ps("bankH", [128, 512], F32, esA)
    psB = bankF.view(bankF.t[:, 0:256])
    psAtt = bankF.view(bankF.t[:, 256:512].rearrange("p (h t) -> p h t", h=2))
    psBT = bankG.view(bankG.t[:, 0:256].rearrange("p (h t) -> p h t", h=2))
    psO = bankG.view(bankG.t[:, 256:512].rearrange("p (h t) -> p h t", h=2))
    psKV = bankH.view(bankH.t[:, 0:256].rearrange("p (h t) -> p h t", h=2))
    psTr = psT

    P.op("pool", "memset", [], [qTa], ap=qTa.t[:], constant=0.0)
    P.op("pool", "memset", [], [qTb], ap=qTb.t[:], constant=0.0)
    for h in range(2):
        P.op("pool", "memset", [], [Sf[h]], ap=Sf[h].t[:], constant=0.0)
        P.op("pool", "memset", [], [Sb[h]], ap=Sb[h].t[:], constant=0.0)

    def load_x(i):
        P.dma("sp", [], [xt[i % 2]], xt[i % 2], out=xt[i % 2].t[:], in_=xall.t[i * 128:(i + 1) * 128, :])

    load_x(0)
    for i in range(NT):
        sample = (i == NT - 1)
        x = xt[i % 2]
        if i + 1 < NT:
            load_x(i + 1)
        P.op("act", "activation", [x], [junk, ss], out=junk.t[:], in_=x.t[:], func=AF.Square, accum_out=ss.t[:, 0:1])
        P.op("act", "activation", [ss, epsb], [ss], out=ss.t[:, 1:2], in_=ss.t[:, 0:1], func=AF.Ln, scale=1.0 / 1024, bias=epsb.t[:, 0:1])
        P.op("act", "activation", [ss], [ss], out=ss.t[:, 2:3], in_=ss.t[:, 1:2], func=AF.Exp, scale=-0.5)
        P.op("dve", "tensor_scalar", [x, ss], [hb], out=hb.t[:], in0=x.t[:], scalar1=ss.t[:, 2:3], scalar2=None, op0=ALU.mult)
        for j in range(8):
            P.op("pe", "transpose", [hb, ident_b], [psT], out=psT.t[:, j, :], in_=hb.t[:, j * 128:(j + 1) * 128], identity=ident_b.t[:])
        P.op("act", "copy", [psT], [hT], out=hT.t[:], in_=psT.t[:])
        for cc in range(4):
            for j in range(8):
                P.op("pe", "matmul", [hT, Wb], [psZ[cc]], out=psZ[cc].t[:], lhsT=hT.t[:, j, :], rhs=Wb.t[:, j, cc * 512:(cc + 1) * 512],
                     start=(j == 0), stop=(j == 7))
        if DEBUG_STOP == 2:
            P.barrier(); esA.close(); es0.close(); P.finish()
            return nc
        P.op("act", "copy", [psZ[3]], [ko[i % 2]], out=ko[i % 2].t[:], in_=psZ[3].t[:, 256:512])
        P.dma("sp", [ko[i % 2]], [], ko[i % 2], out=k_out.t[i * 128:(i + 1) * 128, :], in_=ko[i % 2].t[:])
        if DEBUG_STOP == 20:
            P.barrier(); esA.close(); es0.close(); P.finish()
            return nc
        P.op("dve", "tensor_copy", [psZ[3]], [qkb], out=qkb.t[:], in_=psZ[3].t[:])
        for (dst, off) in (() if not ENABLE_QK else ((qT, 0), (kT, 256))):
            for hq in range(4):
                P.op("pe", "matmul", [qkb, ident_b], [bankH], out=bankH.t[0:64, hq * 128:(hq + 1) * 128], lhsT=qkb.t[:, off + hq * 64:off + (hq + 1) * 64],
                     rhs=ident_b.t[:], start=True, stop=True)
            for hq in range(4):
                P.op("dve", "tensor_copy", [bankH], [dst], out=dst.t[0:64, hq, i * 128:(i + 1) * 128], in_=bankH.t[0:64, hq * 128:(hq + 1) * 128])
        if DEBUG_STOP == 22:
            P.barrier(); esA.close(); es0.close(); P.finish()
            return nc
        P.op("act", "copy", [psZ[2]], [vlin], out=vlin.t[:], in_=psZ[2].t[:, 0:256])
        P.op("act", "copy", [psZ[2]], [vo[i % 2]], out=vo[i % 2].t[:], in_=psZ[2].t[:, 256:512])
        P.dma("sp", [vo[i % 2]], [], vo[i % 2], out=v_out.t[i * 128:(i + 1) * 128, :], in_=vo[i % 2].t[:])
        if DEBUG_STOP == 23:
            P.barrier(); esA.close(); es0.close(); P.finish()
            return nc
        for hq in range(4):
            P.op("dve", "tensor_copy", [vo[i % 2]], [vaug], out=vaug.t[:, i, hq, 0:64], in_=vo[i % 2].t[:, hq * 64:(hq + 1) * 64])
        if DEBUG_STOP == 3:
            P.barrier(); esA.close(); es0.close(); P.finish()
            return nc
        P.op("act", "activation", [psZ[0]], [E0], out=E0.t[:], in_=psZ[0].t[:], func=AF.Exp, scale=-1.0)
        P.op("dve", "tensor_scalar", [E0], [E0], out=E0.t[:], in0=E0.t[:], scalar1=1.0, scalar2=None, op0=ALU.add)
        P.op("dve", "reciprocal", [E0], [E0], out=E0.t[:], in_=E0.t[:])
        P.op("dve", "tensor_tensor", [psZ[0], E0], [qf], out=qf.t[:], in0=psZ[0].t[:, 0:256], in1=E0.t[:, 0:256], op=ALU.mult)
        P.op("dve", "tensor_tensor", [E0, lbt], [ff], out=ff.t[:], in0=E0.t[:, 256:512], in1=lbt.t[:, 1, :], op=ALU.mult)
        P.op("dve", "tensor_tensor", [ff, lbt], [ff], out=ff.t[:], in0=ff.t[:], in1=lbt.t[:, 0, :], op=ALU.add)
        P.op("act", "activation", [ff], [gg], out=gg.t[:], in_=ff.t[:], func=AF.Ln)
        P.op("dve", "tensor_scalar", [ff], [kk], out=kk.t[:], in0=ff.t[:], scalar1=-1.0, scalar2=1.0, op0=ALU.mult, op1=ALU.add)
        P.op("act", "activation", [psZ[1]], [E1], out=E1.t[:], in_=psZ[1].t[:], func=AF.Exp, scale=-1.0)
        P.op("dve", "tensor_scalar", [E1], [E1], out=E1.t[:], in0=E1.t[:], scalar1=1.0, scalar2=None, op0=ALU.add)
        P.op("dve", "reciprocal", [E1], [E1], out=E1.t[:], in_=E1.t[:])
        P.op("dve", "tensor_tensor", [psZ[1], E1], [sgl], out=sgl.t[:], in0=psZ[1].t[:, 0:256], in1=E1.t[:, 0:256], op=ALU.mult)
        P.op("dve", "tensor_tensor", [psZ[1], E1], [sag], out=sag.t[:, i, :], in0=psZ[1].t[:, 256:512], in1=E1.t[:, 256:512], op=ALU.mult)
        P.op("dve", "tensor_tensor", [sgl, glin_s], [glsg], out=glsg.t[:], in0=sgl.t[:], in1=glin_s.t[:], op=ALU.mult)
        if DEBUG_STOP == 4:
            P.barrier(); esA.close(); es0.close(); P.finish()
            return nc
        tri = tri_s if sample else tri_p
        P.op("pe", "matmul", [cst, gg], [psB], out=psB.t[:], lhsT=tri, rhs=gg.t[:], start=True, stop=True)
        for h in range(2):
            P.op("pe", "matmul", [cst, gg], [psBT], out=psBT.t[:, h, :], lhsT=gg.t[:, h * 128:(h + 1) * 128], rhs=tri, start=True, stop=True)
        P.op("act", "activation", [psB], [EB], out=EB.t[:, 0, :], in_=psB.t[:], func=AF.Exp)
        P.op("act", "activation", [psB], [EB], out=EB.t[:, 1, :], in_=psB.t[:], func=AF.Exp, scale=-1.0)
        P.op("dve", "tensor_tensor", [qf, EB], [qkt], out=qkt.t[:, 0, :], in0=qf.t[:], in1=EB.t[:, 0, :], op=ALU.mult)
        P.op("dve", "tensor_tensor", [kk, EB], [qkt], out=qkt.t[:, 1, :], in0=kk.t[:], in1=EB.t[:, 1, :], op=ALU.mult)
        if not sample:
            for h in range(2):
                for c2 in range(2):
                    P.op("act", "activation", [psBT], [ebl], out=ebl.t[:, h, c2:c2 + 1], in_=psBT.t[:, h, 64 * c2 + 63:64 * c2 + 64], func=AF.Exp)
        for h in range(2):
            P.op("pe", "transpose", [qkt, ident_b], [psTr], out=psTr.t[:, 2 * h, :], in_=qkt.t[:, 0, h * 128:(h + 1) * 128], identity=ident_b.t[:])
            P.op("pe", "transpose", [qkt, ident_b], [psTr], out=psTr.t[:, 2 * h + 1, :], in_=qkt.t[:, 1, h * 128:(h + 1) * 128], identity=ident_b.t[:])
        for h4 in range(4):
            P.op("act", "copy", [psTr], [qkT], out=qkT.t[:, h4, :], in_=psTr.t[:, h4, :])
        if not sample:
            for h in range(2):
                P.op("dve", "tensor_copy", [psTr], [qTa], out=qTa.t[:, h, 0:64], in_=psTr.t[:, 2 * h, 0:64])
                P.op("dve", "tensor_copy", [psTr], [qTb], out=qTb.t[:, h, 64:128], in_=psTr.t[:, 2 * h, 64:128])
            for c2 in range(2):
                P.op("dve", "scalar_tensor_tensor", [kk, cst, EB], [ktm], out=ktm.t[:, c2, :], in0=kk.t[:], scalar=cst.t[:, 544 + c2:545 + c2],
                     in1=EB.t[:, 1, :], op0=ALU.mult, op1=ALU.mult)
        for h in range(2):
            P.op("pe", "matmul", [qkT], [psAtt], out=psAtt.t[:, h, :], lhsT=qkT.t[:, 2 * h + 1, :], rhs=qkT.t[:, 2 * h, :], start=True, stop=True)
        for h in range(2):
            P.op("dve", "tensor_tensor", [psAtt, cst], [attm], out=attm.t[:, h, :], in0=psAtt.t[:, h, :], in1=tri, op=ALU.mult)
        if DEBUG_STOP == 5:
            P.barrier(); esA.close(); es0.close(); P.finish()
            return nc
        if not sample:
            for h in range(2):
                hs = slice(h * 128, (h + 1) * 128)
                P.op("pe", "matmul", [attm, vlin], [psO], out=psO.t[:, h, :], lhsT=attm.t[:, h, :], rhs=vlin.t[:, hs], start=True, stop=False)
                for c2 in range(2):
                    qm = qTa if c2 == 0 else qTb
                    P.op("pe", "matmul", [qm, Sb[h]], [psO], out=psO.t[:, h, :], lhsT=qm.t[:, h, :], rhs=Sb[h].t[:], start=False, stop=(c2 == 1))
                    P.op("pe", "matmul", [ktm, vlin], [psKV], out=psKV.t[:, h, :], lhsT=ktm.t[:, c2, hs], rhs=vlin.t[:, hs], start=True, stop=True)
                    P.op("dve", "tensor_tensor", [psKV, Sf[h]], [tmpS[h]], out=tmpS[h].t[:], in0=psKV.t[:, h, :], in1=Sf[h].t[:], op=ALU.add)
                    P.op("act", "activation", [tmpS[h], ebl], [Sf[h]], out=Sf[h].t[:], in_=tmpS[h].t[:], func=AF.Copy, scale=ebl.t[:, h, c2:c2 + 1])
                    P.op("dve", "tensor_scalar", [tmpS[h], ebl], [Sb[h]], out=Sb[h].t[:], in0=tmpS[h].t[:], scalar1=ebl.t[:, h, c2:c2 + 1], scalar2=None, op0=ALU.mult)
            if i == NT - 2:
                for h in range(2):
                    P.dma("sp", [Sf[h]], [], Sf[h], out=stp_out.t[h, :, :], in_=Sf[h].t[:])
        else:
            esS = contextlib.ExitStack()
            S0 = [P.sb("S0_%d" % k, [128, 2, 128], F32, esS) for k in range(4)]
            S0b = [P.sb("S0b_%d" % k, [128, 2, 128], BF16, esS) for k in range(4)]
            Sn = [P.sb("Sn_%d" % k, [128, 2, 128], F32, esS) for k in range(4)]
            vm = [P.sb("vm_%d" % k, [128, 256], BF16, esS) for k in range(4)]
            eblS = P.sb("eblS", [128, 2, 32], F32, esS)
            oiT = P.sb("oiT", [128, 2, 128], F32, esS)
            psOI = psB
            for h in range(2):
                P.op("act", "activation", [psBT], [eblS], out=eblS.t[:, h, :], in_=psBT.t[:, h, 3:128:4], func=AF.Exp)
            for b in range(32):
                k4 = b % 4
                P.dma("sp", [], [S0[k4]], S0[k4], out=S0[k4].t[:], in_=state_in.t[2 * b:2 * b + 2, :, :].rearrange("h k v -> k h v"))
                P.op("act", "copy", [S0[k4]], [S0b[k4]], out=S0b[k4].t[:], in_=S0[k4].t[:])
                P.op("dve", "tensor_scalar", [vlin, cst], [vm[k4]], out=vm[k4].t[:], in0=vlin.t[:], scalar1=bmask[:, b:b + 1], scalar2=None, op0=ALU.mult)
                for h in range(2):
                    hs = slice(h * 128, (h + 1) * 128)
                    P.op("pe", "matmul", [S0b[k4], qkT], [psOI], out=psOI.t[:, h * 128 + 4 * b:h * 128 + 4 * b + 4], lhsT=S0b[k4].t[:, h, :],
                         rhs=qkT.t[:, 2 * h, 4 * b:4 * b + 4], start=True, stop=True)
                    P.op("pe", "matmul", [qkt, vm[k4]], [psKV], out=psKV.t[:, h, :], lhsT=qkt.t[:, 1, hs], rhs=vm[k4].t[:, hs], start=True, stop=True)
                    P.op("dve", "tensor_tensor", [psKV, S0[k4]], [tmpS[h]], out=tmpS[h].t[:], in0=psKV.t[:, h, :], in1=S0[k4].t[:, h, :], op=ALU.add)
                    P.op("act", "activation", [tmpS[h], eblS], [Sn[k4]], out=Sn[k4].t[:, h, :], in_=tmpS[h].t[:], func=AF.Copy, scale=eblS.t[:, h, b:b + 1])
                P.dma("sp", [Sn[k4]], [], Sn[k4], out=sts_out.t[2 * b:2 * b + 2, :, :].rearrange("h k v -> k h v"), in_=Sn[k4].t[:])
            for h in range(2):
                P.op("act", "copy", [psOI], [oiT], out=oiT.t[:, h, :], in_=psOI.t[:, h * 128:(h + 1) * 128])
            for h in range(2):
                hs = slice(h * 128, (h + 1) * 128)
                P.op("pe", "matmul", [attm, vlin], [psO], out=psO.t[:, h, :], lhsT=attm.t[:, h, :], rhs=vlin.t[:, hs], start=True, stop=True)
            for h in range(2):
                P.op("pe", "transpose", [oiT, ident_f], [psAtt], out=psAtt.t[:, h, :], in_=oiT.t[:, h, :], identity=ident_f.t[:])
            esS.close()
        if DEBUG_STOP == 6:
            P.barrier(); esA.close(); es0.close(); P.finish()
            return nc
        if sample:
            osum = P.sb("osum", [128, 2, 128], F32, esA)
            for h in range(2):
                P.op("dve", "tensor_copy", [psAtt], [osum], out=osum.t[:, h, :], in_=psAtt.t[:, h, :])
                P.op("dve", "tensor_tensor", [psO, osum], [osum], out=osum.t[:, h, :], in0=psO.t[:, h, :], in1=osum.t[:, h, :], op=ALU.add)
            osrc = osum
        else:
            osrc = psO
        for h in range(2):
            P.op("act", "activation", [osrc], [junk, osq], out=junk.t[:, 0:128], in_=osrc.t[:, h, :], func=AF.Square, accum_out=osq.t[:, h:h + 1])
        P.op("act", "activation", [osq, epsb], [osq], out=osq.t[:], in_=osq.t[:], func=AF.Ln, scale=1.0 / 128, bias=epsb.t[:, 0:1])
        P.op("act", "activation", [osq], [osq], out=osq.t[:], in_=osq.t[:], func=AF.Exp, scale=-0.5)
        for h in range(2):
            hs = slice(h * 128, (h + 1) * 128)
            P.op("dve", "scalar_tensor_tensor", [osrc, osq, glsg], [olg], out=olg.t[:, hs], in0=osrc.t[:, h, :], scalar=osq.t[:, h:h + 1],
                 in1=glsg.t[:, hs], op0=ALU.mult, op1=ALU.mult)
        for h in range(2):
            P.op("pe", "transpose", [olg, ident_b], [psTr], out=psTr.t[:, h, :], in_=olg.t[:, h * 128:(h + 1) * 128], identity=ident_b.t[:])
        om = omT[i % 2]
        for h in range(2):
            P.op("act", "copy", [psTr], [om], out=om.t[:, h, :], in_=psTr.t[:, h, :])
        ibk, ibap = ib_tile(0, i)
        P.dma("sp", [om], [ibk], ibk, out=ibap, in_=om.t[:])
    P.barrier()
    esA.close()
    es0.close()


    if ENABLE_C:
        esC = contextlib.ExitStack()
        relb_s = P.sb("relb_s", [33, 4], F32, esC)
        oh_s = P.sb("oh_s", [33, TABW], F32, esC)
        rep = P.sb("rep", [33, 128], F32, esC)
        ones33 = P.sb("ones33", [33, 128], F32, esC)
        tabS = P.sb("tabS", [128, TABW], F32, esC)
        Bsk = [P.sb("Bsk%d" % h, [128, WB], F32, esC) for h in range(4)]
        pm_s = P.sb("pm_s", [128, 768], F32, esC)
        ksum = P.sb("ksum", [64, 64], F32, esC)
        kmTb = P.sb("kmTb", [64, 64], BF16, esC)
        g2 = P.sb("g2", [128, 16], F32, esC)
        m8 = P.sb("m8", [128, 8], F32, esC)
        selt = P.sb("selt", [128, 16], F32, esC)
        negpad = P.sb("negpad", [128, 80], BF16, esC)
        tmpL = [P.sb("tmpL%d" % k, [128, 512], F32, esC) for k in range(2)]
        PT = [P.sb("PT%d" % k, [128, 512], BF16, esC) for k in range(2)]
        oatt = [P.sb("oatt%d" % k, [128, 4, 256], BF16, esC) for k in range(2)]
        rec = P.sb("rec", [128, 4], F32, esC)
        omA = [P.sb("omA%d" % k, [128, 2, 128], BF16, esC) for k in range(2)]
        psS = [P.ps("psS%d" % k, [128, 512], F32, esC) for k in range(2)]
        psOa = [P.ps("psOa%d" % k, [128, 65], F32, esC) for k in range(4)]
        psG = P.ps("psG", [128, 512], F32, esC)
        psT2 = P.ps("psT2", [128, 2, 128], BF16, esC)
        psTab = psG
        P.dma("sp", [], [relb_s], relb_s, out=relb_s.t[0:32, :], in_=relb.t[:, :])
        P.op("pool", "memset", [], [relb_s], ap=relb_s.t[32:33, :], constant=-30000.0)
        P.op("pool", "memset", [], [ones33], ap=ones33.t[:], constant=1.0)
        P.op("pool", "memset", [], [negpad], ap=negpad.t[:], constant=0.0)
        P.dma("sp", [], [oh_s], oh_s, out=oh_s.t[:], in_=onehot.t[:, :])
        P.dma("sp", [], [pm_s], pm_s, out=pm_s.t[:], in_=pmask.t[:, :])
        CW = TABW // 4
        for h in range(4):
            P.op("dve", "tensor_scalar", [ones33, relb_s], [rep], out=rep.t[:], in0=ones33.t[:], scalar1=relb_s.t[:, h:h + 1], scalar2=None, op0=ALU.mult)
            for c4 in range(4):
                P.op("pe", "matmul", [rep, oh_s], [psTab], out=psTab.t[:, 0:CW], lhsT=rep.t[:], rhs=oh_s.t[:, c4 * CW:(c4 + 1) * CW], start=True, stop=True)
                P.op("act", "copy", [psTab], [tabS], out=tabS.t[:, c4 * CW:(c4 + 1) * CW], in_=psTab.t[:, 0:CW])
            P.dma("sp", [tabS], [btab], btab, out=btab.t[h, :, :], in_=tabS.t[:])
            skew = bass.AP(tensor=btab.t, offset=h * 128 * TABW + 127, ap=[[TABW - 1, 128], [1, WB]])
            P.dma("sp", [btab], [Bsk[h]], Bsk[h], out=Bsk[h].t[:], in_=skew)
        for h in range(4):
            for pg in range(16):
                j0 = min(2048 - 128 * pg, WB - 4)
                P.op("dve", "tensor_copy", [Bsk[h]], [Bsall], out=Bsall.t[:, h * 64 + pg * 4:h * 64 + pg * 4 + 4], in_=Bsk[h].t[:, j0:j0 + 4])
        ohn_s = P.sb("ohn_s", [33, 512], F32, esC)
        P.dma("sp", [], [ohn_s], ohn_s, out=ohn_s.t[:], in_=ohn.t[:, :])
        for h in range(4):
            for t4 in range(4):
                P.op("pe", "matmul", [ohn_s, relb_s], [psG], out=psG.t[:, h * 4 + t4:h * 4 + t4 + 1], lhsT=ohn_s.t[:, t4 * 128:(t4 + 1) * 128],
                     rhs=relb_s.t[:, h:h + 1], start=True, stop=True)
        P.op("act", "copy", [psG], [Bnn], out=Bnn.t[:], in_=psG.t[:, 0:16])
        P.op("pool", "memset", [], [ksum], ap=ksum.t[:], constant=0.0)
        for h in range(4):
            for n in range(NBLK):
                P.op("dve", "tensor_reduce", [kT], [ksum], out=ksum.t[0:64, h * 16 + n:h * 16 + n + 1], in_=kT.t[0:64, h, n * 256:(n + 1) * 256],
                     axis=AX.X, op=ALU.add)
        P.op("act", "activation", [ksum], [kmTb], out=kmTb.t[:], in_=ksum.t[:], func=AF.Copy, scale=1.0 / 256)
        for h in range(4):
            for qt in range(NP // 128):
                own = qt // 2
                cs = slice(qt * 128, (qt + 1) * 128)
                P.op("pe", "matmul", [qT, kmTb], [psG], out=psG.t[:, 0:16], lhsT=qT.t[0:64, h, cs], rhs=kmTb.t[0:64, h * 16:(h + 1) * 16], start=True, stop=True)
                P.op("dve", "tensor_tensor", [psG, pm_s], [g2], out=g2.t[:], in0=psG.t[:, 0:16], in1=pm_s.t[:, own * 48:own * 48 + 16], op=ALU.add)
                P.op("dve", "max", [g2], [m8], out=m8.t[:], in_=g2.t[:])
                P.op("dve", "tensor_scalar", [g2, m8], [selt], out=selt.t[:], in0=g2.t[:], scalar1=m8.t[:, 2:3], scalar2=None, op0=ALU.is_ge)
                P.op("dve", "tensor_tensor", [selt, pm_s], [selt], out=selt.t[:], in0=selt.t[:], in1=pm_s.t[:, own * 48 + 16:own * 48 + 32], op=ALU.mult)
                P.op("dve", "scalar_tensor_tensor", [selt, pm_s], [negpad], out=negpad.t[:, 64:80], in0=selt.t[:], scalar=32768.0,
                     in1=pm_s.t[:, own * 48 + 32:own * 48 + 48], op0=ALU.mult, op1=ALU.add)
                P.op("pe", "matmul", [negpad, ident_b], [psG], out=psG.t[0:80, 128:256], lhsT=negpad.t[:, 0:80], rhs=ident_b.t[:], start=True, stop=True)
                P.op("dve", "tensor_copy", [psG], [qT], out=qT.t[64:80, h, cs], in_=psG.t[64:80, 128:256])
        chunks = [(q0, min(512, NP - q0)) for q0 in range(0, NP, 512)]
        it = 0

        def emit_qk(itn, h, q0, ncols, kt):
            k0 = kt * 128
            f0 = max(0, k0 - q0)
            pss = psS[itn % 2]
            P.op("pe", "matmul", [kT, qT], [pss], out=pss.t[:, f0:ncols], lhsT=kT.t[0:80, h, k0:k0 + 128], rhs=qT.t[0:80, h, q0 + f0:q0 + ncols],
                 start=True, stop=True)

        for (q0, ncols) in chunks:
            nsub = ncols // 128
            oa = oatt[(q0 // 512) % 2]
            for h in range(4):
                nkt = (q0 + ncols) // 128
                emit_qk(it, h, q0, ncols, 0)
                for kt in range(nkt):
                    k0 = kt * 128
                    f0 = max(0, k0 - q0)
                    pss = psS[it % 2]
                    pt = PT[it % 2]
                    tl = tmpL[it % 2]
                    if kt + 1 < nkt:
                        emit_qk(it + 1, h, q0, ncols, kt + 1)
                    it += 1
                    if q0 - k0 <= 896:
                        P.op("dve", "scalar_tensor_tensor", [pss, Bsk[h]], [tl], out=tl.t[:, f0:ncols], in0=pss.t[:, f0:ncols], scalar=0.125,
                             in1=Bsk[h].t[:, f0 + q0 - k0:ncols + q0 - k0], op0=ALU.mult, op1=ALU.add)
                        P.op("act", "activation", [tl], [pt], out=pt.t[:, f0:ncols], in_=tl.t[:, f0:ncols], func=AF.Exp)
                    else:
                        P.op("act", "activation", [pss, Bsk[h]], [pt], out=pt.t[:, f0:ncols], in_=pss.t[:, f0:ncols], func=AF.Exp, scale=0.125,
                             bias=Bsk[h].t[:, WB - 1:WB])
                    for qs in range(f0 // 128, nsub):
                        P.op("pe", "matmul", [pt, vaug], [psOa[qs]], out=psOa[qs].t[:], lhsT=pt.t[:, qs * 128:(qs + 1) * 128], rhs=vaug.t[:, kt, h, :],
                             start=(kt == 0), stop=(kt == q0 // 128 + qs))
                for qs in range(nsub):
                    tile_i = q0 // 128 + qs
                    pso = psOa[qs]
                    P.op("dve", "reciprocal", [pso], [rec], out=rec.t[:, qs:qs + 1], in_=pso.t[:, 64:65])
                    P.op("dve", "scalar_tensor_tensor", [pso, rec, sag], [oa], out=oa.t[:, qs, h * 64:(h + 1) * 64], in0=pso.t[:, 0:64],
                         scalar=rec.t[:, qs:qs + 1], in1=sag.t[:, tile_i, h * 64:(h + 1) * 64], op0=ALU.mult, op1=ALU.mult)
            for qs in range(nsub):
                tile_i = q0 // 128 + qs
                om = omA[tile_i % 2]
                for pr in range(2):
                    P.op("pe", "transpose", [oa, ident_b], [psT2], out=psT2.t[:, pr, :], in_=oa.t[:, qs, pr * 128:(pr + 1) * 128], identity=ident_b.t[:])
                for pr in range(2):
                    P.op("act", "copy", [psT2], [om], out=om.t[:, pr, :], in_=psT2.t[:, pr, :])
                ibk, ibap = ib_tile(256, tile_i)
                P.dma("sp", [om], [ibk], ibk, out=ibap, in_=om.t[:])
        P.barrier()
        esC.close()


    if ENABLE_C and ENABLE_D:
        esD = contextlib.ExitStack()
        pt_i = P.sb("pt_i", [128, 512], I32, esD)
        pt_f = P.sb("pt_f", [128, 512], F32, esD)
        idx = P.sb("idx", [128, 512], I32, esD)
        negb = P.sb("negb", [128, 32], F32, esD)
        onesb = P.sb("onesb", [128, 1], BF16, esD)
        kpg = [P.sb("kpg%d" % k, [128, 256], F32, esD) for k in range(8)]
        vpg = [P.sb("vpg%d" % k, [128, 256], F32, esD) for k in range(8)]
        kbp = [P.sb("kbp%d" % k, [128, 256], BF16, esD) for k in range(8)]
        vbuf = [P.sb("vbuf%d" % k, [128, 16, 256], BF16, esD) for k in range(2)]
        kTs = [P.sb("kTs%d" % k, [80, 16, 512], BF16, esD) for k in range(2)]
        ksS = P.sb("ksS", [64, 64], F32, esD)
        kmS = P.sb("kmS", [64, 32], F32, esD)
        kmSb = P.sb("kmSb", [64, 32], BF16, esD)
        gs = P.sb("gs", [4, 8], F32, esD)
        m8s = P.sb("m8s", [4, 8], F32, esD)
        sels = P.sb("sels", [4, 8], F32, esD)
        negs = P.sb("negs", [4, 80], BF16, esD)
        tls = P.sb("tls", [128, 256], F32, esD)
        PTs = P.sb("PTs", [128, 256], BF16, esD)
        tns = P.sb("tns", [128, 16], F32, esD)
        PTn = P.sb("PTn", [128, 16], BF16, esD)
        recs = P.sb("recs", [4, 4], F32, esD)
        onb = P.sb("onb", [4, 256], BF16, esD)
        sagT = P.sb("sagT", [128, 2, 128], BF16, esD)
        omS = P.sb("omS", [128, 2, 128], BF16, esD)
        psK = P.ps("psK", [128, 512], F32, esD)
        psKS = P.ps("psKS", [128, 512], F32, esD)
        psGd = P.ps("psGd", [128, 512], F32, esD)
        psSs = P.ps("psSs", [128, 512], F32, esD)
        psOs = P.ps("psOs", [128, 512], F32, esD)
        psOT = P.ps("psOT", [128, 2, 128], F32, esD)
        psTd = P.ps("psTd", [128, 2, 128], BF16, esD)
        P.dma("sp", [], [pt_i], pt_i, out=pt_i.t[:], in_=ptab.t[:, :])
        P.op("dve", "tensor_copy", [pt_i], [pt_f], out=pt_f.t[:], in_=pt_i.t[:])
        P.op("dve", "tensor_scalar", [pt_f, cst], [pt_f], out=pt_f.t[:], in0=pt_f.t[:], scalar1=128.0, scalar2=cst.t[:, 546:547], op0=ALU.mult, op1=ALU.add)
        P.op("dve", "tensor_copy", [pt_f], [idx], out=idx.t[:], in_=pt_f.t[:])
        P.op("dve", "tensor_scalar", [cst], [negb], out=negb.t[:], in0=bmask, scalar1=30000.0, scalar2=-30000.0, op0=ALU.mult, op1=ALU.add)
        P.op("pool", "memset", [], [onesb], ap=onesb.t[:], constant=1.0)
        P.op("pool", "memset", [], [negs], ap=negs.t[:], constant=0.0)
        for k in range(2):
            P.dma("sp", [], [kTs[k]], kTs[k], out=kTs[k].t[64:80, :, :], in_=kind2.t[:, :].rearrange("n (g c) -> n g c", g=16))
        for pr in range(2):
            P.op("pe", "transpose", [sag, ident_b], [psTd], out=psTd.t[:, pr, :], in_=sag.t[:, NT - 1, pr * 128:(pr + 1) * 128], identity=ident_b.t[:])
        for pr in range(2):
            P.op("act", "copy", [psTd], [sagT], out=sagT.t[:, pr, :], in_=psTd.t[:, pr, :])

        def gather(dst, src, col):
            def fn(eng, sem):
                eng.indirect_dma_start(out=dst.t[:, :], out_offset=None, in_=src.t[:, :],
                                       in_offset=bass.IndirectOffsetOnAxis(ap=idx.t[:, col:col + 1], axis=0)).then_inc(sem, 16)
                return 1
            P.dma("pool", [idx], [dst], dst, fn=fn)

        if DEBUG_STOP == 30:
            P.barrier(); esD.close(); P.finish()
            return nc
        for b in range(32):
            kt_b = kTs[b % 2]
            vb = vbuf[b % 2]
            qcols = slice(NP + 4 * b, NP + 4 * b + 4)
            for pg in range(16):
                kp, vp, kb = kpg[pg % 8], vpg[pg % 8], kbp[pg % 8]
                gather(kp, cache_k, b * 16 + pg)
                gather(vp, cache_v, b * 16 + pg)
                P.op("act", "copy", [kp], [kb], out=kb.t[:], in_=kp.t[:])
                P.op("dve", "tensor_copy", [vp], [vb], out=vb.t[:, pg, :], in_=vp.t[:])
                for h in range(4):
                    P.op("pe", "matmul", [kb, ident_b], [psK], out=psK.t[0:64, h * 128:(h + 1) * 128], lhsT=kb.t[:, h * 64:(h + 1) * 64], rhs=ident_b.t[:],
                         start=True, stop=True)
                P.op("dve" if pg % 2 == 0 else "act", "tensor_copy" if pg % 2 == 0 else "copy", [psK], [kt_b], out=kt_b.t[0:64, pg, :], in_=psK.t[0:64, :])
                for h in range(4):
                    P.op("pe", "matmul", [kb, onesb], [psKS], out=psKS.t[0:64, h * 16 + pg:h * 16 + pg + 1], lhsT=kb.t[:, h * 64:(h + 1) * 64], rhs=onesb.t[:],
                         start=True, stop=True)
            if DEBUG_STOP == 31 and b == 0:
                P.barrier(); esD.close(); P.finish()
                return nc
            P.op("act", "copy", [psKS], [ksS], out=ksS.t[:], in_=psKS.t[0:64, 0:64])
            P.op("dve", "tensor_tensor", [ksS], [kmS], out=kmS.t[:], in0=ksS.t[:, 0:64:2], in1=ksS.t[:, 1:64:2], op=ALU.add)
            P.op("act", "activation", [kmS], [kmSb], out=kmSb.t[:], in_=kmS.t[:], func=AF.Copy, scale=1.0 / 256)
            for h in range(4):
                P.op("pe", "matmul", [qT, kmSb], [psGd], out=psGd.t[0:4, h * 8:(h + 1) * 8], lhsT=qT.t[0:64, h, qcols], rhs=kmSb.t[0:64, h * 8:(h + 1) * 8],
                     start=True, stop=True)
                P.op("dve", "tensor_copy", [psGd], [gs], out=gs.t[:], in_=psGd.t[0:4, h * 8:(h + 1) * 8])
                P.op("dve", "max", [gs], [m8s], out=m8s.t[:], in_=gs.t[:])
                P.op("dve", "tensor_scalar", [gs, m8s], [sels], out=sels.t[:], in0=gs.t[:], scalar1=m8s.t[:, 2:3], scalar2=None, op0=ALU.is_ge)
                P.op("dve", "tensor_scalar", [sels], [negs], out=negs.t[:, 64:72], in0=sels.t[:], scalar1=32768.0, scalar2=-32768.0, op0=ALU.mult, op1=ALU.add)
                P.op("pe", "matmul", [negs, ident_b], [psGd], out=psGd.t[0:80, 64:68], lhsT=negs.t[0:4, 0:80], rhs=ident_b.t[0:4, 0:4], start=True, stop=True)
                P.op("dve", "tensor_copy", [psGd], [qT], out=qT.t[64:80, h, qcols], in_=psGd.t[64:80, 64:68])
            if DEBUG_STOP == 32 and b == 0:
                P.barrier(); esD.close(); P.finish()
                return nc
            for h in range(4):
                for pg in range(16):
                    P.op("pe", "matmul", [kt_b, qT], [psSs], out=psSs.t[:, h * 64 + pg * 4:h * 64 + pg * 4 + 4], lhsT=kt_b.t[0:80, pg, h * 128:(h + 1) * 128],
                         rhs=qT.t[0:80, h, qcols], start=True, stop=True)
                P.op("pe", "matmul", [kT, qT], [psSs], out=psSs.t[:, 256 + h * 4:256 + h * 4 + 4], lhsT=kT.t[0:80, h, NP:NTOK], rhs=qT.t[0:80, h, qcols],
                     start=True, stop=True)
            P.op("dve", "scalar_tensor_tensor", [psSs, Bsall], [tls], out=tls.t[:], in0=psSs.t[:, 0:256], scalar=0.125, in1=Bsall.t[:], op0=ALU.mult, op1=ALU.add)
            P.op("act", "activation", [tls], [PTs], out=PTs.t[:], in_=tls.t[:], func=AF.Exp)
            P.op("dve", "scalar_tensor_tensor", [psSs, Bnn], [tns], out=tns.t[:], in0=psSs.t[:, 256:272], scalar=0.125, in1=Bnn.t[:], op0=ALU.mult, op1=ALU.add)
            P.op("dve", "tensor_scalar", [tns, negb], [tns], out=tns.t[:], in0=tns.t[:], scalar1=negb.t[:, b:b + 1], scalar2=None, op0=ALU.add)
            P.op("act", "activation", [tns], [PTn], out=PTn.t[:], in_=tns.t[:], func=AF.Exp)
            if DEBUG_STOP == 33 and b == 0:
                P.barrier(); esD.close(); P.finish()
                return nc
            for h in range(4):
                for pg in range(16):
                    P.op("pe", "matmul", [PTs, vb], [psOs], out=psOs.t[0:4, h * 65:h * 65 + 64], lhsT=PTs.t[:, h * 64 + pg * 4:h * 64 + pg * 4 + 4],
                         rhs=vb.t[:, pg, h * 64:(h + 1) * 64], start=(pg == 0), stop=False)
                P.op("pe", "matmul", [PTn, vaug], [psOs], out=psOs.t[0:4, h * 65:h * 65 + 64], lhsT=PTn.t[:, h * 4:h * 4 + 4], rhs=vaug.t[:, NT - 1, h, 0:64],
                     start=False, stop=True)
                for pg in range(16):
                    P.op("pe", "matmul", [PTs, onesb], [psOs], out=psOs.t[0:4, h * 65 + 64:h * 65 + 65], lhsT=PTs.t[:, h * 64 + pg * 4:h * 64 + pg * 4 + 4],
                         rhs=onesb.t[:], start=(pg == 0), stop=False)
                P.op("pe", "matmul", [PTn, vaug], [psOs], out=psOs.t[0:4, h * 65 + 64:h * 65 + 65], lhsT=PTn.t[:, h * 4:h * 4 + 4], rhs=vaug.t[:, NT - 1, h, 64:65],
                     start=False, stop=True)
            for h in range(4):
                P.op("dve", "reciprocal", [psOs], [recs], out=recs.t[:, h:h + 1], in_=psOs.t[0:4, h * 65 + 64:h * 65 + 65])
                P.op("dve", "tensor_scalar", [psOs, recs], [onb], out=onb.t[:, h * 64:(h + 1) * 64], in0=psOs.t[0:4, h * 65:h * 65 + 64], scalar1=recs.t[:, h:h + 1],
                     scalar2=None, op0=ALU.mult)
            if DEBUG_STOP == 36 and b == 0:
                P.barrier(); esD.close(); P.finish()
                return nc
            for pr in range(2):
                P.op("pe", "matmul", [onb, ident_b], [psOT], out=psOT.t[:, pr, 4 * b:4 * b + 4], lhsT=onb.t[0:4, pr * 128:(pr + 1) * 128], rhs=ident_b.t[0:4, 0:4],
                     start=True, stop=True)
            if DEBUG_STOP == 34 and b == 0:
                P.barrier(); esD.close(); P.finish()
                return nc
            if DEBUG_STOP == 35 and b == 3:
                P.barrier(); esD.close(); P.finish()
                return nc
        for pr in range(2):
            P.op("dve", "tensor_tensor", [psOT, sagT], [omS], out=omS.t[:, pr, :], in0=psOT.t[:, pr, :], in1=sagT.t[:, pr, :], op=ALU.mult)
        ibk, ibap = ib_tile(256, NT - 1)
        P.dma("sp", [omS], [ibk], ibk, out=ibap, in_=omS.t[:])
        P.barrier()
        esD.close()

    if not ENABLE_C:
        esZ = contextlib.ExitStack()
        zt = P.sb("zt", [128, 2, 128], BF16, esZ)
        P.op("pool", "memset", [], [zt], ap=zt.t[:], constant=0.0)
        for i in range(NT):
            ibk, ibap = ib_tile(256, i)
            P.dma("sp", [zt], [ibk], ibk, out=ibap, in_=zt.t[:])
        P.barrier()
        esZ.close()

    if ENABLE_C and not ENABLE_D:
        esZ = contextlib.ExitStack()
        zt = P.sb("zt2", [128, 2, 128], BF16, esZ)
        P.op("pool", "memset", [], [zt], ap=zt.t[:], constant=0.0)
        ibk, ibap = ib_tile(256, NT - 1)
        P.dma("sp", [zt], [ibk], ibk, out=ibap, in_=zt.t[:])
        P.barrier()
        esZ.close()

    P.barrier()
    cc_sem = P.es.enter_context(nc.semaphore("cc_sem"))
    for k in range(NCH):
        nc.gpsimd.collective_compute("AllGather", ALU.bypass, replica_groups=[list(g) for g in GROUPS],
                                     ins=[ibs[k].t.ap().opt()], outs=[obs[k].t.ap().opt()]).then_inc(cc_sem)
        nc.gpsimd.wait_ge(cc_sem, k + 1)
    for e in P.eng:
        P.eng[e].wait_ge(cc_sem, NCH)
    esE = contextlib.ExitStack()
    Wo = P.sb("Wo", [128, 8, 1024], BF16, esE)
    wst2 = [P.sb("wst2_%d" % i, [128, 1024], F32, esE) for i in range(2)]
    gpost_s = P.sb("gpost_s", [128, 1024], F32, esE)
    obt = [P.sb("obt%d" % i, [128, 8, 128], BF16, esE) for i in range(2)]
    xt2 = [P.sb("xt2_%d" % i, [128, 1024], F32, esE) for i in range(2)]
    yt = [P.sb("yt%d" % i, [128, 1024], F32, esE) for i in range(2)]
    junk2 = P.sb("junk2", [128, 512], BF16, esE)
    ssq = P.sb("ssq", [128, 4], F32, esE)
    psYa = [P.ps("psY%d" % i, [128, 512], F32, esE) for i in range(4)]
    P.dma("sp", [], [gpost_s], gpost_s, out=gpost_s.t[:], in_=gpost.t[:, :])
    for j in range(8):
        w = wst2[j % 2]
        P.dma("sp", [], [w], w, out=w.t[:], in_=w_out.t[j * 128:(j + 1) * 128, :])
        if j % 2 == 0:
            P.op("dve", "tensor_copy", [w], [Wo], out=Wo.t[:, j, :], in_=w.t[:])
        else:
            P.op("act", "copy", [w], [Wo], out=Wo.t[:, j, :], in_=w.t[:])

    def load_tail(i):
        obk, c0 = obs[i // CHT], (i % CHT) * 128
        P.dma("sp", [obk], [obt[i % 2]], obt[i % 2], out=obt[i % 2].t[:], in_=obk.t[:, c0:c0 + 128].rearrange("(j p) t -> p j t", p=128))
        P.dma("sp", [], [xt2[i % 2]], xt2[i % 2], out=xt2[i % 2].t[:], in_=xall.t[i * 128:(i + 1) * 128, :])

    load_tail(0)
    for i in range(NT):
        if i + 1 < NT:
            load_tail(i + 1)
        o_t, x_t, y_t = obt[i % 2], xt2[i % 2], yt[i % 2]
        psY = psYa[2 * (i % 2):2 * (i % 2) + 2]
        for cc in range(2):
            for j in range(8):
                P.op("pe", "matmul", [o_t, Wo], [psY[cc]], out=psY[cc].t[:], lhsT=o_t.t[:, j, :], rhs=Wo.t[:, j, cc * 512:(cc + 1) * 512],
                     start=(j == 0), stop=(j == 7))
            P.op("act", "activation", [psY[cc]], [junk2, ssq], out=junk2.t[:], in_=psY[cc].t[:], func=AF.Square, accum_out=ssq.t[:, cc:cc + 1])
        P.op("dve", "tensor_tensor", [ssq], [ssq], out=ssq.t[:, 2:3], in0=ssq.t[:, 0:1], in1=ssq.t[:, 1:2], op=ALU.add)
        P.op("act", "activation", [ssq, epsb], [ssq], out=ssq.t[:, 3:4], in_=ssq.t[:, 2:3], func=AF.Ln, scale=1.0 / 1024, bias=epsb.t[:, 0:1])
        P.op("act", "activation", [ssq], [ssq], out=ssq.t[:, 2:3], in_=ssq.t[:, 3:4], func=AF.Exp, scale=-0.5)
        for cc in range(2):
            cs = slice(cc * 512, (cc + 1) * 512)
            P.op("dve", "scalar_tensor_tensor", [psY[cc], ssq, gpost_s], [y_t], out=y_t.t[:, cs], in0=psY[cc].t[:], scalar=ssq.t[:, 2:3],
                 in1=gpost_s.t[:, cs], op0=ALU.mult, op1=ALU.mult)
        P.op("dve", "tensor_tensor", [y_t, x_t], [y_t], out=y_t.t[:], in0=y_t.t[:], in1=x_t.t[:], op=ALU.add)
        P.dma("sp", [y_t], [], y_t, out=y_out.t[i * 128:(i + 1) * 128, :], in_=y_t.t[:])
    P.barrier()
    esE.close()

    P.finish()
    return nc


def _host_consts():
    c = np.zeros((128, 1024), np.float32)
    s = np.arange(128)[:, None]
    t = np.arange(128)[None, :]
    c[:, 0:128] = ((s // 64 == t // 64) & (s <= t)).astype(np.float32)
    c[:, 128:256] = ((s // 4 == t // 4) & (s <= t)).astype(np.float32)
    c[:, 256:384] = np.eye(128, dtype=np.float32)
    c[:, 512:544] = (s // 4 == np.arange(32)[None, :]).astype(np.float32)
    c[:, 544] = (np.arange(128) < 64).astype(np.float32)
    c[:, 545] = (np.arange(128) >= 64).astype(np.float32)
    c[:, 546] = np.arange(128).astype(np.float32)
    return c


def host_const_inputs():
    NP = NTOK - 128
    kind = np.zeros((16, NTOK), np.float32)
    for n in range(NP // 256):
        kind[n, n * 256:(n + 1) * 256] = 1.0
    kind[8, NP:] = 1.0
    u = np.arange(TABW)
    d = u - 127
    onehot = np.zeros((33, TABW), np.float32)
    bk = t5_bucket_np(d)
    onehot[bk[d >= 0], u[d >= 0]] = 1.0
    onehot[32, u[d < 0]] = 1.0
    pm = np.zeros((16, 48), np.float32)
    n = np.arange(16)
    for o in range(16):
        pm[o, 0:16] = np.where(n < o, 0.0, -1e30)
        pm[o, 16:32] = (n < o).astype(np.float32)
        pm[o, 32:48] = np.where(n == o, 0.0, -32768.0)
    pmask = np.ascontiguousarray(np.broadcast_to(pm.reshape(1, 768), (128, 768)))
    ohn = np.zeros((33, 4, 128), np.float32)
    for t4 in range(4):
        for p in range(128):
            dd = t4 - (p % 4)
            if dd < 0:
                ohn[32, t4, p] = 1.0
            else:
                ohn[int(t5_bucket_np(np.array([dd]))[0]), t4, p] = 1.0
    kind2 = np.zeros((16, 16, 512), np.float32)
    for pg in range(16):
        kind2[pg // 2, pg, :] = 1.0
    return {"kind": kind.astype(ml_dtypes.bfloat16), "onehot": onehot, "pmask": pmask,
            "ohn": ohn.reshape(33, 512), "kind2": kind2.reshape(16, 16 * 512).astype(ml_dtypes.bfloat16)}


def kernel(x_prompt, x_sample, cache_k, cache_v, state_hgrn, page_table, w_in, w_out,
           norm_pre, norm_post, norm_lin_out, lin_lower_bound, rel_bias):
    x_prompt = np.asarray(x_prompt, np.float32)
    x_sample = np.asarray(x_sample, np.float32)
    cache_k = np.asarray(cache_k, np.float32)
    cache_v = np.asarray(cache_v, np.float32)
    state_hgrn = np.asarray(state_hgrn, np.float32)
    page_table = np.asarray(page_table, np.int32)
    w_in = np.asarray(w_in, np.float32)[0]
    w_out = np.asarray(w_out, np.float32)[0]
    norm_pre = np.asarray(norm_pre, np.float32)[0]
    norm_post = np.asarray(norm_post, np.float32)[0]
    norm_lin_out = np.asarray(norm_lin_out, np.float32)[0]
    llb = np.asarray(lin_lower_bound, np.float32)
    rel_bias = np.asarray(rel_bias, np.float32)

    nc = build_program()
    consts = _host_consts()
    hc = host_const_inputs()
    in_maps = []
    for c in range(8):
        b, hh = c // 2, c % 2
        lh = [2 * hh, 2 * hh + 1]
        ah = list(range(4 * hh, 4 * hh + 4))
        lcols = lambda base: np.concatenate([np.arange(base + H * 128, base + (H + 1) * 128) for H in lh])
        acols = lambda base: np.concatenate([np.arange(base + A * 64, base + (A + 1) * 64) for A in ah])
        cols = np.concatenate([lcols(0), lcols(512), lcols(1536), acols(3584), lcols(1024), acols(3072), acols(2048), acols(2560)])
        xall = np.concatenate([x_prompt[b], x_sample[32 * b:32 * b + 32].reshape(128, 1024)], axis=0)
        lbc = np.concatenate([llb[0, lcols(0)], llb[1, lcols(0)]])
        wrows = np.concatenate([np.arange(0, 256), 512 + np.arange(0, 256), 256 + np.arange(0, 256), 768 + np.arange(0, 256)])
        m = {
            "xall": np.ascontiguousarray(xall),
            "w_in": np.ascontiguousarray(w_in[:, cols]),
            "w_out": np.ascontiguousarray(w_out[wrows]),
            "gpre": np.ascontiguousarray(norm_pre.reshape(8, 128).T),
            "gpost": np.ascontiguousarray(np.broadcast_to(norm_post, (128, 1024))),
            "glin": np.ascontiguousarray(np.broadcast_to(np.tile(norm_lin_out, 2), (128, 256))),
            "llb": np.ascontiguousarray(np.broadcast_to(lbc, (128, 512))),
            "relb": np.ascontiguousarray(rel_bias[:, ah]),
            "onehot": hc["onehot"],
            "consts": consts,
            "kind": hc["kind"],
            "pmask": hc["pmask"],
            "ohn": hc["ohn"],
            "kind2": hc["kind2"],
            "cache_k": np.ascontiguousarray(cache_k[0][:, :, ah, :]).reshape(2560 * 128, 256),
            "cache_v": np.ascontiguousarray(cache_v[0][:, :, ah, :]).reshape(2560 * 128, 256),
            "ptab": np.ascontiguousarray(np.broadcast_to(page_table[32 * b:32 * b + 32].reshape(512), (128, 512))),
            "state_in": np.ascontiguousarray(state_hgrn[0, 32 * b:32 * b + 32][:, lh]).reshape(64, 128, 128),
        }
        in_maps.append(m)
    res = run_bass_kernel_spmd(nc, in_maps, core_ids=list(range(8))).results

    y_p = np.zeros((4, 4096, 1024), np.float32)
    y_s = np.zeros((128, 4, 1024), np.float32)
    k_p = np.zeros((1, 4, 4096, 8, 64), np.float32)
    v_p = np.zeros((1, 4, 4096, 8, 64), np.float32)
    st_p = np.zeros((1, 4, 4, 128, 128), np.float32)
    k_s = np.zeros((1, 128, 4, 8, 64), np.float32)
    v_s = np.zeros((1, 128, 4, 8, 64), np.float32)
    st_s = np.zeros((1, 128, 4, 128, 128), np.float32)
    for c in range(8):
        b, hh = c // 2, c % 2
        r = res[c]
        kn = r["k_new"].reshape(NTOK, 4, 64)
        vn = r["v_new"].reshape(NTOK, 4, 64)
        k_p[0, b, :, 4 * hh:4 * hh + 4] = kn[:4096]
        v_p[0, b, :, 4 * hh:4 * hh + 4] = vn[:4096]
        k_s[0, 32 * b:32 * b + 32, :, 4 * hh:4 * hh + 4] = kn[4096:].reshape(32, 4, 4, 64)
        v_s[0, 32 * b:32 * b + 32, :, 4 * hh:4 * hh + 4] = vn[4096:].reshape(32, 4, 4, 64)
        st_p[0, b, 2 * hh:2 * hh + 2] = r["st_p"]
        st_s[0, 32 * b:32 * b + 32, 2 * hh:2 * hh + 2] = r["st_s"].reshape(32, 2, 128, 128)
        y = r["y"]
        half = slice(2048 * hh, 2048 * hh + 2048)
        y_p[b, half] = y[:4096][half]
        y_s[32 * b + 16 * hh:32 * b + 16 * hh + 16] = y[4096:].reshape(32, 4, 1024)[16 * hh:16 * hh + 16]
    return (y_p, y_s, k_p, v_p, st_p, k_s, v_s, st_s)
```
